# Optimizing a Trainium2 kernel written in Bass

```python
import jax, jax.numpy as jnp
from jax import lax
import numpy as np

D_MODEL = 1024
BATCH = 4
SEQ = 8192
DEPTH = 1

CHUNK = 64
N_MEM = 256
ATT_HEADS = 8
HEAD_DIM = 64
D_ATT = ATT_HEADS * HEAD_DIM
D_LRU = D_MODEL - D_ATT
LRU_BLOCKS = 8
LRU_BLOCK = D_LRU // LRU_BLOCKS
CONV_W = 4
LRU_C = 8.0
LEFT_CHUNKS = 8
BAND = (LEFT_CHUNKS + 1) * CHUNK
MAX_REL = 128
X_HEADS = 4
X_HEAD_DIM = D_MODEL // X_HEADS
D_FF = -(-8 * D_MODEL // (3 * 256)) * 256
D_IN = 3 * D_ATT + 2 * D_LRU
EPS = 1e-6

kernel_name = "hymba_chunk_attn_rglru_xmem_swiglu"


def rmsnorm(x, g):
    x32 = x.astype(jnp.float32)
    y = x32 * lax.rsqrt(jnp.mean(x32 * x32, axis=-1, keepdims=True) + EPS)
    return (y * g.astype(jnp.float32)).astype(x.dtype)


def chunk_attention(q, k, v, rel_bias):
    B, S, H, Dh = q.shape
    nc = S // CHUNK
    pad = LEFT_CHUNKS * CHUNK
    kp = jnp.pad(k, ((0, 0), (pad, 0), (0, 0), (0, 0)))
    vp = jnp.pad(v, ((0, 0), (pad, 0), (0, 0), (0, 0)))
    rel = (LEFT_CHUNKS * CHUNK + np.arange(CHUNK)[:, None]) - np.arange(BAND)[None, :]
    idx = np.clip(rel, -MAX_REL, MAX_REL) + MAX_REL
    bias = rel_bias[:, idx].astype(jnp.float32)
    qc = q.reshape(B, nc, CHUNK, H, Dh).transpose(1, 0, 2, 3, 4)
    scale = HEAD_DIM ** -0.5
    key_off = jnp.arange(BAND, dtype=jnp.int32)

    def one_chunk(args):
        c, qb = args
        kb = lax.dynamic_slice_in_dim(kp, c * CHUNK, BAND, axis=1)
        vb = lax.dynamic_slice_in_dim(vp, c * CHUNK, BAND, axis=1)
        s = jnp.einsum('bqhd,bkhd->bhqk', qb, kb).astype(jnp.float32) * scale + bias
        valid = (c - LEFT_CHUNKS) * CHUNK + key_off >= 0
        s = jnp.where(valid, s, -1e30)
        p = jax.nn.softmax(s, axis=-1).astype(vb.dtype)
        return jnp.einsum('bhqk,bkhd->bqhd', p, vb)

    o = lax.map(one_chunk, (jnp.arange(nc, dtype=jnp.int32), qc))
    return o.transpose(1, 0, 2, 3, 4).reshape(B, S, H * Dh)


def causal_conv(u, w, b):
    S = u.shape[1]
    up = jnp.pad(u, ((0, 0), (CONV_W - 1, 0), (0, 0)))
    out = up[:, 0:S] * w[0]
    for j in range(1, CONV_W):
        out = out + up[:, j:j + S] * w[j]
    return out + b


def _lin_combine(left, right):
    a_l, b_l = left
    a_r, b_r = right
    return a_r * a_l, a_r * b_l + b_r


def rg_lru(u, w_rg, b_rg, w_ig, b_ig, L):
    B, S, _ = u.shape
    ub = u.reshape(B, S, LRU_BLOCKS, LRU_BLOCK)
    r = jax.nn.sigmoid(jnp.einsum('bsnc,ncd->bsnd', ub, w_rg).reshape(B, S, D_LRU) + b_rg)
    i = jax.nn.sigmoid(jnp.einsum('bsnc,ncd->bsnd', ub, w_ig).reshape(B, S, D_LRU) + b_ig)
    log_a = -LRU_C * r.astype(jnp.float32) * jax.nn.softplus(-L.astype(jnp.float32))
    a = jnp.exp(log_a)
    mult = jnp.sqrt(jnp.maximum(-jnp.expm1(2.0 * log_a), 0.0))
    b = mult * (i * u).astype(jnp.float32)
    _, h = lax.associative_scan(_lin_combine, (a, b), axis=1)
    return h.astype(u.dtype)


def setup_inputs(seed: int = 0) -> dict:
    key = jax.random.key(seed)
    ks = jax.random.split(key, 32)
    f32 = jnp.float32

    def w(k, shape, fan_in):
        return jax.random.normal(k, shape, f32) * (fan_in ** -0.5)

    def gain(k, shape):
        return 1.0 + 0.05 * jax.random.normal(k, shape, f32)

    def small(k, shape, s=0.01):
        return s * jax.random.normal(k, shape, f32)

    a0 = jax.random.uniform(ks[11], (DEPTH, D_LRU), f32, 0.9, 0.999) ** (1.0 / LRU_C)
    lru_L = jnp.log(a0) - jnp.log1p(-a0)
    return {
        "x": jax.random.normal(ks[0], (BATCH, SEQ, D_MODEL), f32),
        "mem": jax.random.normal(ks[1], (BATCH, N_MEM, D_MODEL), f32),
        "g_mix": gain(ks[2], (DEPTH, D_MODEL)),
        "w_in": w(ks[3], (DEPTH, D_MODEL, D_IN), D_MODEL),
        "rel_bias": 0.1 * jax.random.normal(ks[4], (DEPTH, ATT_HEADS, 2 * MAX_REL + 1), f32),
        "conv_w": w(ks[5], (DEPTH, CONV_W, D_LRU), CONV_W),
        "conv_b": small(ks[6], (DEPTH, D_LRU)),
        "w_rg": w(ks[7], (DEPTH, LRU_BLOCKS, LRU_BLOCK, LRU_BLOCK), LRU_BLOCK),
        "b_rg": small(ks[8], (DEPTH, D_LRU)),
        "w_ig": w(ks[9], (DEPTH, LRU_BLOCKS, LRU_BLOCK, LRU_BLOCK), LRU_BLOCK),
        "b_ig": small(ks[10], (DEPTH, D_LRU)),
        "lru_L": lru_L,
        "g_out_attn": gain(ks[12], (DEPTH, D_ATT)),
        "g_out_lru": gain(ks[13], (DEPTH, D_LRU)),
        "w_out": w(ks[14], (DEPTH, D_ATT + D_LRU, D_MODEL), D_ATT + D_LRU),
        "g_cross": gain(ks[15], (DEPTH, D_MODEL)),
        "g_mem": gain(ks[16], (DEPTH, D_MODEL)),
        "wq_c": w(ks[17], (DEPTH, D_MODEL, D_MODEL), D_MODEL),
        "wk_c": w(ks[18], (DEPTH, D_MODEL, D_MODEL), D_MODEL),
        "wv_c": w(ks[19], (DEPTH, D_MODEL, D_MODEL), D_MODEL),
        "wo_c": w(ks[20], (DEPTH, D_MODEL, D_MODEL), D_MODEL),
        "g_ffn": gain(ks[21], (DEPTH, D_MODEL)),
        "w_gate": w(ks[22], (DEPTH, D_MODEL, D_FF), D_MODEL),
        "w_up": w(ks[23], (DEPTH, D_MODEL, D_FF), D_MODEL),
        "w_down": w(ks[24], (DEPTH, D_FF, D_MODEL), D_FF),
        "g_final": gain(ks[25], (D_MODEL,)),
    }


def reference(x, mem, g_mix, w_in, rel_bias, conv_w, conv_b, w_rg, b_rg, w_ig, b_ig, lru_L,
              g_out_attn, g_out_lru, w_out, g_cross, g_mem, wq_c, wk_c, wv_c, wo_c,
              g_ffn, w_gate, w_up, w_down, g_final):
    B, S, _ = x.shape
    M = mem.shape[1]
    splits = [D_ATT, 2 * D_ATT, 3 * D_ATT, 3 * D_ATT + D_LRU]
    for l in range(DEPTH):
        h = rmsnorm(x, g_mix[l])
        proj = h @ w_in[l]
        q, k, v, xu, gu = jnp.split(proj, splits, axis=-1)
        att = chunk_attention(q.reshape(B, S, ATT_HEADS, HEAD_DIM),
                              k.reshape(B, S, ATT_HEADS, HEAD_DIM),
                              v.reshape(B, S, ATT_HEADS, HEAD_DIM), rel_bias[l])
        xu = causal_conv(xu, conv_w[l], conv_b[l])
        rec = rg_lru(xu, w_rg[l], b_rg[l], w_ig[l], b_ig[l], lru_L[l]) * jax.nn.gelu(gu)
        merged = jnp.concatenate([rmsnorm(att, g_out_attn[l]), rmsnorm(rec, g_out_lru[l])], axis=-1)
        x = x + merged @ w_out[l]

        hc = rmsnorm(x, g_cross[l])
        mn = rmsnorm(mem, g_mem[l])
        qx = (hc @ wq_c[l]).reshape(B, S, X_HEADS, X_HEAD_DIM)
        kx = (mn @ wk_c[l]).reshape(B, M, X_HEADS, X_HEAD_DIM)
        vx = (mn @ wv_c[l]).reshape(B, M, X_HEADS, X_HEAD_DIM)
        s = jnp.einsum('bshd,bmhd->bhsm', qx, kx).astype(jnp.float32) * (X_HEAD_DIM ** -0.5)
        p = jax.nn.softmax(s, axis=-1).astype(vx.dtype)
        ox = jnp.einsum('bhsm,bmhd->bshd', p, vx).reshape(B, S, D_MODEL)
        x = x + ox @ wo_c[l]

        hf = rmsnorm(x, g_ffn[l])
        x = x + (jax.nn.silu(hf @ w_gate[l]) * (hf @ w_up[l])) @ w_down[l]
    return rmsnorm(x, g_final)
```

```python
import contextlib
import numpy as np
import concourse.bass as bass
import concourse.mybir as mybir
from concourse.bass_utils import run_bass_kernel_spmd

F32 = mybir.dt.float32
BF16 = mybir.dt.bfloat16
AF = mybir.ActivationFunctionType
ALU = mybir.AluOpType

D = 1024
T = 4096
BT = 512
NBLK = T // BT
DFF = 2816
NF = DFF // 128
M = 256
EPS = 1e-6
NCOL = 72
NEG = -30000.0

C_GMIX, C_GCROSS, C_GFFN, C_GMEM, C_GLRU, C_CONVW, C_CONVB, C_BRG, C_BIG, C_L, C_FLAG = 0, 8, 16, 24, 32, 36, 52, 56, 60, 64, 68


class _Op:
    __slots__ = ("idx", "eng", "fn", "deps", "is_dma", "ndma", "stream", "sigval", "cum")

    def __init__(self, idx, eng, fn, deps, is_dma=False, ndma=0, stream=None):
        self.idx = idx; self.eng = eng; self.fn = fn; self.deps = deps
        self.is_dma = is_dma; self.ndma = ndma; self.stream = stream
        self.sigval = None; self.cum = None


class Sched:
    def __init__(self):
        self.ops = []
        self.lw = {}
        self.rd = {}
        self.dma_cum = {}

    def _deps(self, r, w):
        deps = {}
        for k in r:
            i = self.lw.get(k)
            if i is not None:
                deps[i] = True
        for k in w:
            i = self.lw.get(k)
            if i is not None:
                deps.setdefault(i, False)
            for j in self.rd.get(k, {}).values():
                deps.setdefault(j, False)
        return deps

    def _commit(self, idx, who, r, w):
        for k in r:
            self.rd.setdefault(k, {})[who] = idx
        for k in w:
            self.lw[k] = idx
            self.rd[k] = {}

    def op(self, eng, fn, r=(), w=(), x=()):
        r = tuple(r) + ("ARENA",); w = tuple(w) + tuple(x)
        idx = len(self.ops)
        deps = self._deps(r, w)
        for k in x:
            i = self.lw.get(k)
            if i is not None:
                deps[i] = True
        self.ops.append(_Op(idx, eng, fn, deps))
        self._commit(idx, eng, r, w)
        return idx

    def dma(self, q, fn, n, stream, r=(), w=()):
        r = tuple(r) + ("ARENA",); w = tuple(w)
        idx = len(self.ops)
        deps = self._deps(r, w)
        o = _Op(idx, q, fn, deps, True, n, stream)
        self.dma_cum[stream] = self.dma_cum.get(stream, 0) + n
        o.cum = self.dma_cum[stream]
        self.ops.append(o)
        self._commit(idx, ("dma", stream), r, w)
        return idx

    def barrier(self):
        idx = len(self.ops)
        deps = self._deps((), ("ARENA",))
        self.ops.append(_Op(idx, "sp", lambda e: e.nop(), deps))
        self._commit(idx, "sp", (), ("ARENA",))

    def emit(self, block_engines, sems):
        ops = self.ops

        def skip(o, p, raw):
            return (not o.is_dma) and (not p.is_dma) and p.eng == o.eng and (o.eng == "pe" or not raw)

        needed = [False] * len(ops)
        for o in ops:
            for i, raw in o.deps.items():
                p = ops[i]
                if p.is_dma or skip(o, p, raw):
                    continue
                needed[i] = True
        cnt = {}
        for o in ops:
            if (not o.is_dma) and needed[o.idx]:
                cnt[o.eng] = cnt.get(o.eng, 0) + 1
                o.sigval = cnt[o.eng]
        self.stats = dict(cnt)
        per_eng = {}
        for o in ops:
            per_eng.setdefault(o.eng, []).append(o)

        def run(engname, e):
            seen = {}
            for o in per_eng.get(engname, ()):
                waits = {}
                for i, raw in o.deps.items():
                    p = ops[i]
                    if p.is_dma:
                        key = p.stream; val = 16 * p.cum
                    else:
                        if skip(o, p, raw):
                            continue
                        key = p.eng; val = p.sigval
                    if val > waits.get(key, 0):
                        waits[key] = val
                for key, val in waits.items():
                    if val > seen.get(key, 0):
                        seen[key] = val
                        e.wait_ge(sems[key], val)
                if o.fn is None:
                    continue
                ins = o.fn(e)
                if o.is_dma:
                    assert len(ins) == o.ndma, (len(ins), o.ndma)
                    for xx in ins:
                        xx.then_inc(sems[o.stream], 16)
                elif o.sigval is not None:
                    ins.then_inc(sems[o.eng], 1)

        for engname, reg in block_engines.items():
            reg(lambda e, _n=engname: run(_n, e))


class Arena:
    def __init__(self, ap, nwords):
        self.ap = ap; self.n = nwords; self.off = 0

    def alloc_top(self, free_shape, dt):
        n = int(np.prod(free_shape))
        words = n if dt == F32 else (n + 1) // 2
        words = (words + 3) // 4 * 4
        self.top = getattr(self, "top", self.n) - words
        save = self.off
        self.off = self.top
        a = self.alloc(free_shape, dt)
        self.off = save
        return a

    def alloc(self, free_shape, dt):
        n = int(np.prod(free_shape))
        words = n if dt == F32 else (n + 1) // 2
        words = (words + 3) // 4 * 4
        a = self.ap[:, self.off:self.off + words]
        self.off += words
        assert self.off <= self.n, ("arena overflow", self.off, self.n)
        if dt != F32:
            a = a.bitcast(dt)
        a = a[:, 0:n]
        if len(free_shape) == 2:
            a = a.rearrange("p (a b) -> p a b", a=free_shape[0])
        elif len(free_shape) == 3:
            a = a.rearrange("p (a b c) -> p a b c", a=free_shape[0], b=free_shape[1])
        return a


def build(phases=(0, 1, 2, 3, 4)):
    nc = bass.Bass("TRN2", target_bir_lowering=False)

    def din(name, shape):
        return nc.dram_tensor(name, list(shape), F32, kind="ExternalInput").ap()

    x_main = din("x_main", [T, D]); x_prev = din("x_prev", [T, D]); mem_d = din("mem", [M, D])
    cols_d = din("cols", [128, NCOL]); gfin_d = din("g_final", [D]); gatt_d = din("g_out_attn", [512])
    gmix_d = din("g_mix", [D]); gcr_d = din("g_cross", [D]); gffn_d = din("g_ffn", [D]); gmem_d = din("g_mem", [D])
    ident_d = din("ident", [128, 128]); bt_d = din("bias_t", [128, 8 * 5 * 128])
    w_in_d = din("w_in", [D, 2560]); w_out_d = din("w_out", [D, D])
    wrg_d = din("wbd_rg", [4, 128, 128]); wig_d = din("wbd_ig", [4, 128, 128])
    wq_d = din("wq_c", [D, D]); wk_d = din("wk_c", [D, D]); wv_d = din("wv_c", [D, D]); wo_d = din("wo_c", [D, D])
    wg_d = din("w_gate", [D, DFF]); wu_d = din("w_up", [D, DFF]); wd_d = din("w_down", [DFF, D])
    out_d = nc.dram_tensor("out", [T, D], F32, kind="ExternalOutput").ap()
    scr = [nc.dram_tensor(f"scr{i}", [T, D], F32, kind="Internal").ap() for i in range(2)]

    main = [p for p in phases if p in (2, 3, 4)]
    srcs, dsts = {}, {}
    cur = x_main
    for i, p in enumerate(main):
        srcs[p] = cur
        dsts[p] = out_d if i == len(main) - 1 else scr[i % 2]
        cur = dsts[p]

    s = Sched()
    with contextlib.ExitStack() as es:
        NW = 212000 // 4
        arena_t = es.enter_context(nc.sbuf_tensor("arena", [128, NW], F32))
        AR = Arena(arena_t[:, :], NW)
        ps_names = ["T", "X", "P0", "P1", "S0", "S1", "O0", "O1"]
        PSALL = es.enter_context(nc.psum_tensor("psall", [128, 4096], F32))
        PS = {n: PSALL[:, i * 512:(i + 1) * 512] for i, n in enumerate(ps_names)}
        PSP = {"B": PSALL[:, 1024:2048], "C": PSALL[:, 2048:3072], "D": PSALL[:, 3072:4096]}
        class _Sems(dict):
            def __missing__(self, name):
                v = es.enter_context(nc.semaphore(str(name)))
                self[name] = v
                return v
        sems = _Sems()
        block = es.enter_context(nc.Block())

        ident = AR.alloc([128], BF16)
        ones = AR.alloc([128], BF16)
        cols = AR.alloc([NCOL], F32)
        c1 = AR.alloc([4], F32); c2 = AR.alloc([4], F32); epsb = AR.alloc([1], F32); tmp4 = AR.alloc([4], F32)
        mhalf = AR.alloc([1], F32); phalf = AR.alloc([1], F32); negb = AR.alloc([8], F32)
        PERS0 = AR.off
        KxT = AR.alloc([8, M], BF16)
        Vx = AR.alloc([2, D], BF16)
        PERS1 = AR.off

        s.dma("sp", lambda e: [e.dma_start(out=cols, in_=cols_d)], 1, "c_cols", w=["cols"])
        s.dma("pool", lambda e: [e.dma_start(out=ident, in_=ident_d)], 1, "c_ident", w=["ident"])
        s.op("dve", lambda e: e.memset(ones, 1.0), w=["ones"])
        s.op("dve", lambda e: e.memset(epsb, EPS), w=["epsb"])
        s.op("dve", lambda e: e.memset(mhalf, -0.5), w=["mhalf"])
        s.op("dve", lambda e: e.memset(phalf, 0.25), w=["phalf"])
        s.op("dve", lambda e: e.tensor_scalar(out=negb, in0=cols[:, C_BRG:C_BRG + 8], scalar1=0.5, scalar2=0.0, op0=ALU.mult, op1=ALU.add), r=["cols"], w=["negb"])
        s.op("act", lambda e: e.activation(out=tmp4, in_=cols[:, C_L:C_L + 4], func=AF.Exp, scale=-1.0), r=["cols"], w=["tmp4"])
        s.op("act", lambda e: e.activation(out=tmp4, in_=tmp4, func=AF.Ln, bias=1.0, scale=1.0), r=["tmp4"], w=["tmp4"])
        s.op("dve", lambda e: e.tensor_scalar(out=c1, in0=tmp4, scalar1=-8.0, scalar2=None, op0=ALU.mult), r=["tmp4"], w=["c1"])
        s.op("dve", lambda e: e.tensor_scalar(out=c2, in0=tmp4, scalar1=-4.0, scalar2=None, op0=ALU.mult), r=["tmp4"], w=["c2"])

        def load_w(dst, src_d, nk, key, c0=None, c1_=None):
            def fn(e):
                res = []
                for kt in range(nk):
                    src = src_d[kt * 128:(kt + 1) * 128, :] if c0 is None else src_d[kt * 128:(kt + 1) * 128, c0:c1_]
                    res.append(e.dma_start(out=dst[:, kt, :], in_=src))
                return res
            s.dma("pool", fn, nk, "w_" + key, w=[key])

        def load_row(dst, src_d, key):
            s.dma("sp", lambda e: [e.dma_start(out=dst, in_=src_d.partition_broadcast(128))], 1, "c_" + key, w=[key])

        def rstd_pool(ss, inv_n, tmp, rstd, kss, kout):
            s.op("act", lambda e: e.activation(out=tmp, in_=ss, func=AF.Sqrt, bias=epsb, scale=inv_n), r=[kss, "epsb"], w=[(kout, "d")])
            s.op("dve", lambda e: e.reciprocal(out=rstd, in_=tmp), r=[(kout, "d")], w=[kout])

        class NormBufs:
            def __init__(self, tag, grow, gkey, nxn=4):
                self.tag = tag; self.grow = grow; self.gkey = gkey
                self.xs = [AR.alloc([D], F32) for _ in range(2)]
                self.xn = [AR.alloc([D], BF16) for _ in range(nxn)]
                self.st = [[AR.alloc([1], F32) for _ in range(3)] for _ in range(2)]
                self.n = 0

        def norm_pre(nb, xsrc_ap):
            n = nb.n; nb.n += 1
            a = n % 2; j = n % len(nb.xn)
            xs = nb.xs[a]; xn = nb.xn[j]; ss, sd, rstd = nb.st[a]
            kx = f"xs{nb.tag}{a}"; kn = f"xn{nb.tag}{j}"; kst = f"st{nb.tag}{a}"
            s.dma("sp", lambda e: [e.dma_start(out=xs, in_=xsrc_ap)], 1, "ld" + kx, w=[kx])
            s.op("act", lambda e: e.activation(out=xn, in_=xs, func=AF.Square, accum_out=ss), r=[kx], w=[kn, kst + "s"])
            rstd_pool(ss, 1.0 / D, sd, rstd, kst + "s", kst + "r")
            s.op("dve", lambda e: e.scalar_tensor_tensor(out=xn, in0=xs, scalar=rstd, in1=nb.grow, op0=ALU.mult, op1=ALU.mult),
                 r=[kx, kst + "r", nb.gkey], w=[kn])
            return j

        def norm_tr(nb, j, hT, col0, tbank, evac_eng):
            xn = nb.xn[j]; kn = f"xn{nb.tag}{j}"
            tp = PS[tbank][:, :].bitcast(BF16)

            def tr(e):
                ins = None
                for kt in range(8):
                    ins = e.transpose(tp[:, kt * 128:(kt + 1) * 128], xn[:, kt * 128:(kt + 1) * 128], ident)
                return ins
            s.op("pe", tr, r=[kn, "ident"], x=[tbank])
            tp3 = tp.rearrange("p (a b) -> p a b", a=8)
            if evac_eng == "act":
                s.op("act", lambda e: e.activation(out=hT[:, :, col0:col0 + 128], in_=tp3, func=AF.Copy), x=[tbank], w=[("hT", id(hT), col0)])
            else:
                s.op("dve", lambda e: e.tensor_copy(out=hT[:, :, col0:col0 + 128], in_=tp3), x=[tbank], w=[("hT", id(hT), col0)])

        def hT_keys(hT, ncol=BT):
            return [("hT", id(hT), c) for c in range(0, ncol, 128)]

        TB = ["T", "X"]

        Wg = AR.alloc_top([8, DFF], BF16); Wu = AR.alloc_top([8, DFF], BF16)
        WTOP = AR.top
        ffn_w_loaded = [False]

        def load_ffn_w():
            if not ffn_w_loaded[0]:
                load_w(Wg, wg_d, 8, "Wg"); load_w(Wu, wu_d, 8, "Wu")
                ffn_w_loaded[0] = True

        def phase4(xsrc, xdst):
            AR.off = PERS0
            Wd = AR.alloc([NF, D], BF16)
            gffn = AR.alloc([D], F32); gfin = AR.alloc([D], F32)
            nb = NormBufs("f", gffn, "gffn")
            xr = [AR.alloc([D], F32) for _ in range(2)]
            junk = AR.alloc([D], BF16)
            hT = AR.alloc([8, BT], BF16)
            aT = AR.alloc([NF, BT], BF16)
            sg = [AR.alloc([BT], F32) for _ in range(2)]
            fst = [[AR.alloc([1], F32) for _ in range(3)] for _ in range(2)]
            print("arena words used (phase4):", AR.off, "of", AR.n)
            load_row(gffn, gffn_d, "gffn"); load_row(gfin, gfin_d, "gfin")
            assert AR.off <= WTOP, (AR.off, WTOP)
            load_ffn_w(); load_w(Wd, wd_d, NF, "Wd")
            hk = hT_keys(hT)
            slots = {}

            def A_pre(b, tt):
                r0 = b * BT + tt * 128
                slots[(b, tt)] = norm_pre(nb, xsrc[r0:r0 + 128, :])

            def A_tr(b):
                for tt in range(4):
                    norm_tr(nb, slots[(b, tt)], hT, tt * 128, TB[tt % 2], "act")

            def GU(b):
                for f in range(NF):
                    pg = "P0" if f % 2 == 0 else "S0"
                    pu = "P1" if f % 2 == 0 else "S1"

                    def mm(e, W, bank, f=f):
                        ins = None
                        for kt in range(8):
                            ins = e.matmul(PS[bank][:, :], lhsT=W[:, kt, f * 128:(f + 1) * 128], rhs=hT[:, kt, :], start=(kt == 0), stop=(kt == 7))
                        return ins
                    s.op("pe", lambda e, f=f, pg=pg, mm=mm: mm(e, Wg, pg), r=["Wg"] + hk, x=[pg])
                    s.op("pe", lambda e, f=f, pu=pu, mm=mm: mm(e, Wu, pu), r=["Wu"] + hk, x=[pu])
                    sgi = sg[f % 2]
                    s.op("act", lambda e, pg=pg, sgi=sgi: e.activation(out=sgi, in_=PS[pg][:, :], func=AF.Silu), x=[pg], w=[("sg", f % 2)])
                    s.op("dve", lambda e, pu=pu, sgi=sgi, f=f: e.tensor_tensor(out=aT[:, f, :], in0=PS[pu][:, :], in1=sgi, op=ALU.mult),
                         r=[("sg", f % 2)], x=[pu], w=[("aT", f)])
                    if b + 1 < NBLK and f in (3, 8, 13, 18):
                        A_pre(b + 1, (f - 3) // 5)

            def DN(b):
                for tt in range(4):
                    r0 = b * BT + tt * 128
                    i = tt % 2
                    s.dma("sp", lambda e, i=i, r0=r0: [e.dma_start(out=xr[i], in_=xsrc[r0:r0 + 128, :])], 1, f"ldr{i}", w=[("xr", i)])
                    for hf in range(2):
                        bank = "O0" if hf == 0 else "O1"

                        def dm(e, tt=tt, hf=hf, bank=bank):
                            ins = None
                            for f in range(NF):
                                ins = e.matmul(PS[bank][:, :], lhsT=aT[:, f, tt * 128:(tt + 1) * 128], rhs=Wd[:, f, hf * 512:(hf + 1) * 512],
                                               start=(f == 0), stop=(f == NF - 1))
                            return ins
                        s.op("pe", dm, r=["Wd"] + [("aT", f) for f in range(NF)], x=[bank])
                        s.op("dve", lambda e, i=i, hf=hf, bank=bank: e.tensor_tensor(out=xr[i][:, hf * 512:(hf + 1) * 512], in0=PS[bank][:, :],
                                                                                   in1=xr[i][:, hf * 512:(hf + 1) * 512], op=ALU.add),
                             r=[("xr", i)], x=[bank], w=[("xr", i)])
                    ss, sd, rstd = fst[i]
                    s.op("act", lambda e, i=i, ss=ss: e.activation(out=junk, in_=xr[i], func=AF.Square, accum_out=ss), r=[("xr", i)], w=["junk", ("fs", i)])
                    rstd_pool(ss, 1.0 / D, sd, rstd, ("fs", i), ("fr", i))
                    s.op("dve", lambda e, i=i, rstd=rstd: e.scalar_tensor_tensor(out=xr[i], in0=xr[i], scalar=rstd, in1=gfin, op0=ALU.mult, op1=ALU.mult),
                         r=[("xr", i), ("fr", i), "gfin"], w=[("xr", i)])
                    s.dma("pool", lambda e, i=i, r0=r0: [e.dma_start(out=xdst[r0:r0 + 128, :], in_=xr[i])], 1, f"st{i}", r=[("xr", i)], w=[("dst4", r0)])

            for tt in range(4):
                A_pre(0, tt)
            A_tr(0)
            for b in range(NBLK):
                GU(b)
                if b + 1 < NBLK:
                    A_tr(b + 1)
                DN(b)

        def phase0():
            AR.off = PERS1
            Wk = AR.alloc([8, D], BF16); Wv = AR.alloc([8, D], BF16)
            gmem = AR.alloc([D], F32)
            nb = NormBufs("m", gmem, "gmem", nxn=2)
            hTm = AR.alloc([8, M], BF16)
            load_row(gmem, gmem_d, "gmem")
            load_w(Wk, wk_d, 8, "Wk"); load_w(Wv, wv_d, 8, "Wv")
            for mt in range(2):
                j = norm_pre(nb, mem_d[mt * 128:(mt + 1) * 128, :])
                norm_tr(nb, j, hTm, mt * 128, TB[mt], "act")
            hk = hT_keys(hTm, M)
            for f in range(8):
                bank = "P0" if f % 2 == 0 else "P1"

                def mmk(e, f=f, bank=bank):
                    ins = None
                    for kt in range(8):
                        ins = e.matmul(PS[bank][:, 0:M], lhsT=Wk[:, kt, f * 128:(f + 1) * 128], rhs=hTm[:, kt, :], start=(kt == 0), stop=(kt == 7))
                    return ins
                s.op("pe", mmk, r=["Wk"] + hk, x=[bank])
                s.op("act", lambda e, f=f, bank=bank: e.activation(out=KxT[:, f, :], in_=PS[bank][:, 0:M], func=AF.Copy), x=[bank], w=["KxT"])
            for mt in range(2):
                for hf in range(2):
                    bank = "S0" if hf == 0 else "S1"

                    def mmv(e, mt=mt, hf=hf, bank=bank):
                        ins = None
                        for kt in range(8):
                            ins = e.matmul(PS[bank][:, :], lhsT=hTm[:, kt, mt * 128:(mt + 1) * 128], rhs=Wv[:, kt, hf * 512:(hf + 1) * 512], start=(kt == 0), stop=(kt == 7))
                        return ins
                    s.op("pe", mmv, r=["Wv"] + hk, x=[bank])
                    s.op("dve", lambda e, mt=mt, hf=hf, bank=bank: e.tensor_copy(out=Vx[:, mt, hf * 512:(hf + 1) * 512], in_=PS[bank][:, :]), x=[bank], w=["Vx"])

        def phase3(xsrc, xdst):
            AR.off = PERS1
            Wq = AR.alloc([8, D], BF16); Wo = AR.alloc([8, D], BF16)
            gcr = AR.alloc([D], F32)
            nb = NormBufs("c", gcr, "gcr")
            xr = [AR.alloc([D], F32) for _ in range(4)]
            hT = AR.alloc([8, BT], BF16)
            qT = AR.alloc([8, BT], BF16)
            PT = [[AR.alloc([BT], BF16) for _ in range(2)] for _ in range(2)]
            rden = [AR.alloc([BT], F32) for _ in range(2)]
            oxT = AR.alloc([8, BT], BF16)
            load_row(gcr, gcr_d, "gcr")
            load_w(Wq, wq_d, 8, "Wq"); load_w(Wo, wo_d, 8, "Wo")
            assert AR.off <= WTOP, (AR.off, WTOP)
            load_ffn_w()
            hk = hT_keys(hT)
            slots = {}

            def A_pre(b):
                for tt in range(4):
                    r0 = b * BT + tt * 128
                    slots[(b, tt)] = norm_pre(nb, xsrc[r0:r0 + 128, :])

            def A_tr(b):
                for tt in range(4):
                    norm_tr(nb, slots[(b, tt)], hT, tt * 128, TB[tt % 2], "dve")

            def Q(b):
                for f in range(8):
                    bank = "P0" if f % 2 == 0 else "P1"

                    def mq(e, f=f, bank=bank):
                        ins = None
                        for kt in range(8):
                            ins = e.matmul(PS[bank][:, :], lhsT=Wq[:, kt, f * 128:(f + 1) * 128], rhs=hT[:, kt, :], start=(kt == 0), stop=(kt == 7))
                        return ins
                    s.op("pe", mq, r=["Wq"] + hk, x=[bank])
                    s.op("act", lambda e, f=f, bank=bank: e.activation(out=qT[:, f, :], in_=PS[bank][:, :], func=AF.Copy), x=[bank], w=[("qT", f)])

            def ATT(b):
                def SC(h):
                    hp = h % 2
                    for mt in range(2):
                        bank = ("S0", "S1")[mt] if h % 2 == 0 else ("P0", "P1")[mt]

                        def msc(e, h=h, mt=mt, bank=bank):
                            ins = None
                            for dt in range(2):
                                ins = e.matmul(PS[bank][:, :], lhsT=KxT[:, 2 * h + dt, mt * 128:(mt + 1) * 128], rhs=qT[:, 2 * h + dt, :], start=(dt == 0), stop=(dt == 1))
                            return ins
                        s.op("pe", msc, r=["KxT", ("qT", 2 * h), ("qT", 2 * h + 1)], x=[bank])
                        s.op("act", lambda e, hp=hp, mt=mt, bank=bank: e.activation(out=PT[hp][mt], in_=PS[bank][:, :], func=AF.Exp, scale=1.0 / 16.0),
                             x=[bank], w=[("PT", hp, mt)])

                def DPV(h):
                    hp = h % 2

                    def mden(e, hp=hp):
                        ins = None
                        for mt in range(2):
                            ins = e.matmul(PS["O0"][:, :], lhsT=ones, rhs=PT[hp][mt], start=(mt == 0), stop=(mt == 1))
                        return ins
                    s.op("pe", mden, r=["ones", ("PT", hp, 0), ("PT", hp, 1)], x=["O0"])
                    s.op("dve", lambda e, hp=hp: e.reciprocal(out=rden[hp], in_=PS["O0"][:, :]), x=["O0"], w=[("rden", hp)])
                    for dt in range(2):
                        bank = "O1" if dt == 0 else "X"
                        f = 2 * h + dt

                        def mpv(e, f=f, hp=hp, bank=bank):
                            ins = None
                            for mt in range(2):
                                ins = e.matmul(PS[bank][:, :], lhsT=Vx[:, mt, f * 128:(f + 1) * 128], rhs=PT[hp][mt], start=(mt == 0), stop=(mt == 1))
                            return ins
                        s.op("pe", mpv, r=["Vx", ("PT", hp, 0), ("PT", hp, 1)], x=[bank])
                        s.op("dve", lambda e, f=f, hp=hp, bank=bank: e.tensor_tensor(out=oxT[:, f, :], in0=PS[bank][:, :], in1=rden[hp], op=ALU.mult),
                             r=[("rden", hp)], x=[bank], w=[("oxT", f)])

                SC(0)
                for h in range(4):
                    if h + 1 < 4:
                        SC(h + 1)
                    DPV(h)

            def XR(b):
                for tt in range(4):
                    r0 = b * BT + tt * 128
                    s.dma("sp", lambda e, i=tt, r0=r0: [e.dma_start(out=xr[i], in_=xsrc[r0:r0 + 128, :])], 1, f"ldr{tt}", w=[("xr", tt)])

            def OUT(b):
                for tt in range(4):
                    r0 = b * BT + tt * 128
                    i = tt
                    for hf in range(2):
                        bank = "P0" if hf == 0 else "P1"

                        def mo(e, tt=tt, hf=hf, bank=bank):
                            ins = None
                            for f in range(8):
                                ins = e.matmul(PS[bank][:, :], lhsT=oxT[:, f, tt * 128:(tt + 1) * 128], rhs=Wo[:, f, hf * 512:(hf + 1) * 512], start=(f == 0), stop=(f == 7))
                            return ins
                        s.op("pe", mo, r=["Wo"] + [("oxT", f) for f in range(8)], x=[bank])
                        s.op("dve", lambda e, i=i, hf=hf, bank=bank: e.tensor_tensor(out=xr[i][:, hf * 512:(hf + 1) * 512], in0=PS[bank][:, :],
                                                                                   in1=xr[i][:, hf * 512:(hf + 1) * 512], op=ALU.add),
                             r=[("xr", i)], x=[bank], w=[("xr", i)])
                    s.dma("pool", lambda e, i=i, r0=r0: [e.dma_start(out=xdst[r0:r0 + 128, :], in_=xr[i])], 1, f"st{i}", r=[("xr", i)], w=[("dst3", r0)])

            A_pre(0); A_tr(0)
            for b in range(NBLK):
                if b + 1 < NBLK:
                    A_pre(b + 1)
                Q(b)
                if b + 1 < NBLK:
                    A_tr(b + 1)
                XR(b)
                ATT(b)
                OUT(b)

        def phase12(xsrc, xdst):
            AR.off = PERS0
            Win = AR.alloc([8, 2560], BF16); Wout = AR.alloc([8, D], BF16)
            Wrg = AR.alloc([4, 128], BF16); Wig = AR.alloc([4, 128], BF16)
            EB = AR.alloc([4, 5 * 256], BF16)
            gmix = AR.alloc([D], F32); gatt = AR.alloc([512], F32)
            nb = NormBufs("x", gmix, "gmix")
            xr = [AR.alloc([D], F32) for _ in range(2)]
            hT = AR.alloc([8, BT], BF16)
            QP = AR.alloc([4, 2, BT], BF16)
            KT = AR.alloc([4, 1024], BF16)
            VA = AR.alloc([8, 8, 65], BF16)
            XU = AR.alloc([4, 3 + BT], BF16)
            DW = AR.alloc([16, 128], BF16)
            L_u = [AR.alloc([BT], F32) for _ in range(2)]
            L_ub = [AR.alloc([BT], BF16) for _ in range(2)]
            L_r = [AR.alloc([BT], F32) for _ in range(2)]
            L_i = [AR.alloc([BT], F32) for _ in range(2)]
            L_a = [AR.alloc([BT], F32) for _ in range(2)]
            L_h = [AR.alloc([BT], F32) for _ in range(2)]
            state = AR.alloc([4], F32)
            mo0 = AR.off
            gg = AR.alloc([4, BT], F32)
            rec = AR.alloc([4, BT], F32)
            att = [AR.alloc([8, 64], F32) for _ in range(4)]
            mo1 = AR.off
            AR.off = mo0
            for _ in range(2):
                L_u.append(AR.alloc([BT], F32)); L_ub.append(AR.alloc([BT], BF16)); L_r.append(AR.alloc([BT], F32))
                L_i.append(AR.alloc([BT], F32)); L_a.append(AR.alloc([BT], F32)); L_h.append(AR.alloc([BT], F32))
            assert AR.off <= mo1, (AR.off, mo1)
            AR.off = mo1
            sq = AR.alloc([4, BT], BF16)
            recb = AR.alloc([4, BT], BF16)
            rstdB = AR.alloc([BT], F32)
            PTp = [AR.alloc([5 * 256], BF16) for _ in range(2)]
            rc = [AR.alloc([8], F32) for _ in range(4)]
            attb = [AR.alloc([512], BF16) for _ in range(2)]
            attT = [AR.alloc([4, 128], BF16) for _ in range(2)]
            ast = [[AR.alloc([1], F32) for _ in range(3)] for _ in range(2)]
            print("arena words used (phase12):", AR.off, "of", AR.n)

            load_row(gmix, gmix_d, "gmix"); load_row(gatt, gatt_d, "gatt")
            load_w(Win, w_in_d, 8, "Win")
            s.dma("pool", lambda e: [e.dma_start(out=Wrg[:, ct, :], in_=wrg_d[ct]) for ct in range(4)] +
                                    [e.dma_start(out=Wig[:, ct, :], in_=wig_d[ct]) for ct in range(4)] +
                                    [e.dma_start(out=EB[:, p, :], in_=bt_d[:, p * 1280:(p + 1) * 1280]) for p in range(4)], 12, "w_small",
                  w=["Wrg", "Wig", "EB"])
            load_w(Wout, w_out_d, 8, "Wout")
            s.op("act", lambda e: e.activation(out=EB, in_=EB, func=AF.Exp), r=["EB"], w=["EB"])
            s.op("dve", lambda e: e.memset(XU[:, :, 0:3], 0.0), w=[("XUh", ct) for ct in range(4)])

            def mkdw(e):
                ins = None
                for k in range(16):
                    ins = e.tensor_scalar(out=DW[:, k, :], in0=ident, scalar1=cols[:, C_CONVW + k:C_CONVW + k + 1], scalar2=None, op0=ALU.mult)
                return ins
            s.op("dve", mkdw, r=["ident", "cols"], w=["DW"])
            s.op("dve", lambda e: e.memset(state, 0.0), w=[("state", ct) for ct in range(4)])
            s.op("dve", lambda e: e.memset(VA[:, :, :, 64:65], 1.0), w=[("VA", g) for g in range(8)])
            s.op("dve", lambda e: e.memset(QP, 0.0), w=[("QP", ft) for ft in range(4)])
            flag = cols[:, C_FLAG:C_FLAG + 1]
            hk = hT_keys(hT)
            slots = {}
            pbank = ["P0", "P1", "S0", "S1"]
            pc = [0]

            blocks = [("pre" if pb < NBLK - 1 else "prekv", x_prev[pb * BT:(pb + 1) * BT, :], -1) for pb in range(NBLK)]
            blocks += [("main", xsrc[b * BT:(b + 1) * BT, :], b) for b in range(NBLK)]

            def A_pre(bi, tt):
                slots[(bi, tt)] = norm_pre(nb, blocks[bi][1][tt * 128:(tt + 1) * 128, :])

            def A_tr(bi, tt):
                norm_tr(nb, slots[(bi, tt)], hT, tt * 128, TB[tt % 2], "act" if blocks[bi][0] != "main" else "dve")

            def proj_fm(col0, evac):
                bank = pbank[pc[0] % 4]; pc[0] += 1

                def mm(e, col0=col0, bank=bank):
                    ins = None
                    for kt in range(8):
                        ins = e.matmul(PS[bank][:, :], lhsT=Win[:, kt, col0:col0 + 128], rhs=hT[:, kt, :], start=(kt == 0), stop=(kt == 7))
                    return ins
                s.op("pe", mm, r=["Win"] + hk, x=[bank])
                evac(bank)

            def stageB(mode, b, after_half=lambda h: None):
                main_ = mode == "main"
                ringbase = ((4 * b) + 4) % 8 if mode != "pre" else 0

                def projK():
                    for ft in range(4):
                        proj_fm(512 + ft * 128, lambda bank, ft=ft: s.op(
                            "act", lambda e: e.activation(out=KT[:, ft, ringbase * 128:ringbase * 128 + BT], in_=PS[bank][:, :], func=AF.Copy),
                            x=[bank], w=[("KT", ringbase + t) for t in range(4)]))

                def projV():
                    for tt in range(4):
                        bank = pbank[pc[0] % 4]; pc[0] += 1
                        ring = ringbase + tt

                        def mv(e, tt=tt, bank=bank):
                            ins = None
                            for kt in range(8):
                                ins = e.matmul(PS[bank][:, :], lhsT=hT[:, kt, tt * 128:(tt + 1) * 128], rhs=Win[:, kt, 1024:1536], start=(kt == 0), stop=(kt == 7))
                            return ins
                        s.op("pe", mv, r=["Win"] + hk, x=[bank])
                        s.op("dve", lambda e, ring=ring, bank=bank: e.tensor_copy(out=VA[:, ring, :, 0:64], in_=PS[bank][:, :].rearrange("p (h d) -> p h d", h=8)),
                             x=[bank], w=[("VA", ring)])
                    if mode == "prekv":
                        s.op("dve", lambda e: e.tensor_scalar(out=VA[:, 0:4, :, 64:65], in0=VA[:, 0:4, :, 64:65], scalar1=flag, scalar2=None, op0=ALU.mult),
                             r=["cols"], w=[("VA", g) for g in range(4)])
                    if main_ and b == 1:
                        s.op("dve", lambda e: e.memset(VA[:, 0:4, :, 64:65], 1.0), w=[("VA", g) for g in range(4)])

                def projQ():
                    for ft in range(4):
                        def evq(bank, ft=ft):
                            def f(e):
                                e.activation(out=QP[0:64, ft, 0, :], in_=PS[bank][0:64, :], func=AF.Copy, scale=0.125)
                                return e.activation(out=QP[64:128, ft, 1, :], in_=PS[bank][64:128, :], func=AF.Copy, scale=0.125)
                            s.op("act", f, x=[bank], w=[("QP", ft)])
                        proj_fm(ft * 128, evq)

                def projXU():
                    for ct in range(4):
                        proj_fm(1536 + ct * 128, lambda bank, ct=ct: s.op(
                            "act", lambda e: e.activation(out=XU[:, ct, 3:3 + BT], in_=PS[bank][:, :], func=AF.Copy),
                            x=[bank], w=[("XU", ct)]))

                def projGU():
                    for ct in range(4):
                        proj_fm(2048 + ct * 128, lambda bank, ct=ct: s.op(
                            "act", lambda e: e.activation(out=gg[:, ct, :], in_=PS[bank][:, :], func=AF.Gelu_apprx_tanh),
                            x=[bank], w=[("gg", ct)]))

                if main_:
                    sm = {0: 0, 1: 1, 2: 0, 3: 1}
                    projXU(); projGU()
                    lru_front([0, 1], sm)
                    projK()
                    projV()
                    lru_back([0, 1], sm, True)
                    projQ()
                    lru_front([2, 3], sm)
                else:
                    projXU()
                    if mode == "prekv":
                        projK(); projV()

            GB = {0: ("T", "X"), 1: ("O0", "O1"), 2: ("S0", "S1"), 3: ("P0", "P1")}

            def lru_front(cts, smap):
                for ct in cts:
                    j_ = smap[ct]
                    u = L_u[j_]; ub = L_ub[j_]
                    br = GB[j_][0]
                    cb = cols[:, C_CONVB + ct:C_CONVB + ct + 1]

                    def cv(e, ct=ct, br=br):
                        ins = None
                        for j in range(4):
                            ins = e.matmul(PS[br][:, :], lhsT=DW[:, 4 * j + ct, :], rhs=XU[:, ct, j:j + BT], start=(j == 0), stop=(j == 3))
                        return ins
                    s.op("pe", cv, r=[("XU", ct), ("XUh", ct), "DW"], x=[br])
                    s.op("act", lambda e, u=u, br=br, cb=cb: e.activation(out=u, in_=PS[br][:, :], func=AF.Identity, bias=cb, scale=1.0),
                         r=["cols"], x=[br], w=[("u", j_)])
                    s.op("dve", lambda e, ub=ub, br=br, cb=cb: e.tensor_scalar(out=ub, in0=PS[br][:, :], scalar1=cb, scalar2=None, op0=ALU.add),
                         r=["cols"], x=[br], w=[("ub", j_)])
                    s.op("pool", lambda e, ct=ct: e.tensor_copy(out=XU[:, ct, 0:3], in_=XU[:, ct, BT:BT + 3]), r=[("XU", ct)], w=[("XUh", ct)])

            def lru_back(cts, smap, main_, mid=lambda: None):
                gb = GB
                for ct in cts:
                    j_ = smap[ct]
                    ub = L_ub[j_]
                    br, bi_ = gb[j_]
                    s.op("pe", lambda e, ct=ct, ub=ub, br=br: e.matmul(PS[br][:, :], lhsT=Wrg[:, ct, :], rhs=ub, start=True, stop=True),
                         r=["Wrg", ("ub", j_)], x=[br])
                    s.op("pe", lambda e, ct=ct, ub=ub, bi_=bi_: e.matmul(PS[bi_][:, :], lhsT=Wig[:, ct, :], rhs=ub, start=True, stop=True),
                         r=["Wig", ("ub", j_)], x=[bi_])
                for ct in cts:
                    j_ = smap[ct]
                    br, bi_ = gb[j_]
                    s.op("act", lambda e, ct=ct, j_=j_, br=br: e.activation(out=L_r[j_], in_=PS[br][:, :], func=AF.Tanh, bias=negb[:, ct:ct + 1], scale=0.5),
                         r=["negb"], x=[br], w=[("Lr", j_)])
                    s.op("act", lambda e, ct=ct, j_=j_, bi_=bi_: e.activation(out=L_i[j_], in_=PS[bi_][:, :], func=AF.Tanh, bias=negb[:, 4 + ct:5 + ct], scale=0.5),
                         r=["negb"], x=[bi_], w=[("Li", j_)])
                for ct in cts:
                    j_ = smap[ct]
                    s.op("pool", lambda e, j_=j_: e.tensor_tensor(out=L_i[j_], in0=L_i[j_], in1=L_u[j_], op=ALU.mult),
                         r=[("Li", j_), ("u", j_)], w=[("Li", j_)])
                    s.op("pool", lambda e, j_=j_: e.tensor_tensor(out=L_i[j_], in0=L_i[j_], in1=L_u[j_], op=ALU.add),
                         r=[("Li", j_), ("u", j_)], w=[("Li", j_)])
                for ct in cts:
                    j_ = smap[ct]
                    s.op("act", lambda e, ct=ct, j_=j_: e.activation(out=L_a[j_], in_=L_r[j_], func=AF.Exp, scale=c2[:, ct:ct + 1], bias=c2[:, ct:ct + 1]),
                         r=[("Lr", j_), "c2"], w=[("La", j_)])
                    s.op("pool", lambda e, j_=j_: e.tensor_tensor(out=L_r[j_], in0=L_a[j_], in1=L_a[j_], op=ALU.mult), r=[("La", j_)], w=[("Lr", j_)])
                for ct in cts:
                    j_ = smap[ct]
                    s.op("act", lambda e, j_=j_: e.activation(out=L_r[j_], in_=L_r[j_], func=AF.Sqrt, bias=phalf, scale=-0.25),
                         r=[("Lr", j_), "phalf"], w=[("Lr", j_)])
                mid()
                for ct in cts:
                    j_ = smap[ct]

                    s.op("pool", lambda e, j_=j_: e.tensor_tensor(out=L_i[j_], in0=L_i[j_], in1=L_r[j_], op=ALU.mult),
                         r=[("Li", j_), ("Lr", j_)], w=[("Li", j_)])
                    s.op("dve", lambda e, ct=ct, j_=j_: e.tensor_tensor_scan(out=L_h[j_], data0=L_a[j_], data1=L_i[j_], initial=state[:, ct:ct + 1],
                                                                          op0=ALU.mult, op1=ALU.add),
                         r=[("La", j_), ("Li", j_), ("state", ct)], w=[("Lh", j_)])
                    s.op("dve", lambda e, ct=ct, j_=j_: e.tensor_copy(out=state[:, ct:ct + 1], in_=L_h[j_][:, BT - 1:BT]), r=[("Lh", j_)], w=[("state", ct)])
                    if main_:
                        s.op("pool", lambda e, ct=ct, j_=j_: e.tensor_tensor(out=rec[:, ct, :], in0=L_h[j_], in1=gg[:, ct, :], op=ALU.mult),
                             r=[("Lh", j_), ("gg", ct)], w=[("rec", ct)])
                        s.op("pool", lambda e, ct=ct: e.tensor_tensor(out=sq[:, ct, :], in0=rec[:, ct, :], in1=rec[:, ct, :], op=ALU.mult), r=[("rec", ct)], w=[("sq", ct)])

            def lru_norm():
                def mss(e):
                    ins = None
                    for ct in range(4):
                        ins = e.matmul(PS["T"][:, :], lhsT=ones, rhs=sq[:, ct, :], start=(ct == 0), stop=(ct == 3))
                    return ins
                s.op("pe", mss, r=["ones"] + [("sq", ct) for ct in range(4)], x=["T"])
                s.op("act", lambda e: e.activation(out=rstdB, in_=PS["T"][:, :], func=AF.Sqrt, bias=epsb, scale=1.0 / 512.0), r=["epsb"], x=["T"], w=["rstdB"])
                s.op("dve", lambda e: e.reciprocal(out=rstdB, in_=rstdB), r=["rstdB"], w=["rstdB"])
                for ct in range(4):
                    s.op("dve", lambda e, ct=ct: e.scalar_tensor_tensor(out=recb[:, ct, :], in0=rec[:, ct, :], scalar=cols[:, C_GLRU + ct:C_GLRU + ct + 1],
                                                                        in1=rstdB, op0=ALU.mult, op1=ALU.mult),
                         r=[("rec", ct), "rstdB", "cols"], w=[("recb", ct)])

            sbank = ["P0", "P1", "S0", "S1"]
            sc = [0]

            def ATT(b, tt):
                G = 4 * b + tt
                i = tt % 2
                Opair = PSP["D"]
                def QK(ft):
                    pb_ = ft % 2
                    ptk = ("PT", pb_)
                    for grp, rs in enumerate(((0, 1), (2, 3), (4,))):
                        bank = sbank[sc[0] % 4]; sc[0] += 1

                        def fqk(e, rs=rs, bank=bank, ft=ft, G=G, tt=tt):
                            ins = None
                            for li, r in enumerate(rs):
                                ring = (G + r) % 8
                                ins = e.matmul(PS[bank][:, li * 256:(li + 1) * 256].rearrange("p (a b) -> p a b", a=2), lhsT=KT[:, ft, ring * 128:(ring + 1) * 128],
                                               rhs=QP[:, ft, :, tt * 128:(tt + 1) * 128], start=True, stop=True)
                            return ins
                        s.op("pe", fqk, r=[("KT", (G + r) % 8) for r in rs] + [("QP", ft)], x=[bank])
                        ncol = 256 * len(rs)
                        c0 = grp * 512
                        s.op("act", lambda e, bank=bank, pb_=pb_, c0=c0, ncol=ncol: e.activation(out=PTp[pb_][:, c0:c0 + ncol], in_=PS[bank][:, 0:ncol], func=AF.Exp),
                             x=[bank], w=[(ptk, grp)])
                    s.op("dve", lambda e, pb_=pb_, ft=ft: e.tensor_tensor(out=PTp[pb_], in0=PTp[pb_], in1=EB[:, ft, :], op=ALU.mult),
                         r=[(ptk, g_) for g_ in range(3)] + ["EB"], w=[(ptk, g_) for g_ in range(3)])

                def PV(ft):
                    pb_ = ft % 2
                    ptk = ("PT", pb_)
                    for hh in range(2):
                        h = 2 * ft + hh
                        ocol = (h // 4) * 512 + (h % 4) * 65
                        obank = "O0" if h < 4 else "O1"

                        def fpv(e, h=h, hh=hh, ocol=ocol, pb_=pb_, G=G):
                            ins = None
                            for r in range(5):
                                ring = (G + r) % 8
                                ins = e.matmul(Opair[:, ocol:ocol + 65], lhsT=PTp[pb_][:, r * 256 + hh * 128:r * 256 + hh * 128 + 128], rhs=VA[:, ring, h, :],
                                               start=(r == 0), stop=(r == 4))
                            return ins
                        s.op("pe", fpv, r=[(ptk, g_) for g_ in range(3)] + [("VA", (G + r) % 8) for r in range(5)], x=[obank])

                QK(0)
                for ft in range(4):
                    if ft + 1 < 4:
                        QK(ft + 1)
                    PV(ft)
                for bk in range(2):
                    obank = "O0" if bk == 0 else "O1"
                    Ov = PS[obank][:, 0:260].rearrange("p (h d) -> p h d", h=4)
                    rcv = rc[tt][:, 4 * bk:4 * bk + 4].unsqueeze(2)
                    s.op("dve", lambda e, Ov=Ov, rcv=rcv: e.reciprocal(out=rcv, in_=Ov[:, :, 64:65]), x=[obank], w=[("rc", tt, bk)])
                    s.op("dve", lambda e, Ov=Ov, rcv=rcv, bk=bk, tt=tt: e.tensor_tensor(out=att[tt][:, 4 * bk:4 * bk + 4, :], in0=Ov[:, :, 0:64],
                                                                                      in1=rcv.to_broadcast([128, 4, 64]), op=ALU.mult),
                         r=[("rc", tt, bk)], x=[obank], w=[("att", tt, bk)])

            def TAIL(b, tt):
                i = tt % 2
                attf = att[tt].rearrange("p h d -> p (h d)")
                ss, sd, rstd = ast[i]
                s.op("act", lambda e: e.activation(out=attb[i], in_=attf, func=AF.Square, accum_out=ss),
                     r=[("att", tt, 0), ("att", tt, 1)], w=[("attb", i), ("as", i)])
                rstd_pool(ss, 1.0 / 512.0, sd, rstd, ("as", i), ("ar", i))
                s.op("dve", lambda e: e.scalar_tensor_tensor(out=attb[i], in0=attf, scalar=rstd, in1=gatt, op0=ALU.mult, op1=ALU.mult),
                     r=[("att", tt, 0), ("att", tt, 1), ("ar", i), "gatt"], w=[("attb", i)])
            def TAIL_PE(b, tt):
                i = tt % 2
                tp = PS["X"][:, :].bitcast(BF16)

                def tr(e):
                    ins = None
                    for ft in range(4):
                        ins = e.transpose(tp[:, ft * 128:(ft + 1) * 128], attb[i][:, ft * 128:(ft + 1) * 128], ident)
                    return ins
                s.op("pe", tr, r=[("attb", i), "ident"], x=["X"])
                s.op("act", lambda e: e.activation(out=attT[i], in_=tp[:, 0:512].rearrange("p (a b) -> p a b", a=4), func=AF.Copy), x=["X"], w=[("attT", i)])
                r0 = b * BT + tt * 128
                s.dma("sp", lambda e: [e.dma_start(out=xr[i], in_=xsrc[r0:r0 + 128, :])], 1, f"ldr{i}", w=[("xr", i)])
                for hf in range(2):
                    bank = "X" if hf == 0 else "T"

                    def mo(e, hf=hf, bank=bank):
                        ins = None
                        for ft in range(4):
                            e.matmul(PS[bank][:, :], lhsT=attT[i][:, ft, :], rhs=Wout[:, ft, hf * 512:(hf + 1) * 512], start=(ft == 0), stop=False)
                        for ct in range(4):
                            ins = e.matmul(PS[bank][:, :], lhsT=recb[:, ct, tt * 128:(tt + 1) * 128], rhs=Wout[:, 4 + ct, hf * 512:(hf + 1) * 512],
                                           start=False, stop=(ct == 3))
                        return ins
                    s.op("pe", mo, r=["Wout", ("attT", i)] + [("recb", ct) for ct in range(4)], x=[bank])
                    s.op("dve", lambda e, hf=hf, bank=bank: e.tensor_tensor(out=xr[i][:, hf * 512:(hf + 1) * 512], in0=PS[bank][:, :],
                                                                          in1=xr[i][:, hf * 512:(hf + 1) * 512], op=ALU.add),
                         r=[("xr", i)], x=[bank], w=[("xr", i)])
                s.dma("pool", lambda e: [e.dma_start(out=xdst[r0:r0 + 128, :], in_=xr[i])], 1, f"st{i}", r=[("xr", i)], w=[("dst2", r0)])

            nblk = len(blocks)
            sm4 = {0: 0, 1: 1, 2: 2, 3: 3}

            def A_all(bi):
                for tt in range(4):
                    A_pre(bi, tt)
                for tt in range(4):
                    A_tr(bi, tt)

            A_all(0)
            stageB(blocks[0][0], -1)
            lru_front([0, 1, 2, 3], sm4)
            A_all(1)
            for n in range(NBLK):
                def mid(n=n):
                    if n + 1 < NBLK:
                        stageB(blocks[n + 1][0], -1)
                        lru_front([0, 1, 2, 3], sm4)
                lru_back([0, 1, 2, 3], sm4, False, mid)
                if n + 2 <= NBLK:
                    A_all(n + 2)
            s.op("dve", lambda e: e.tensor_scalar(out=state, in0=state, scalar1=flag, scalar2=None, op0=ALU.mult),
                 r=["cols"] + [("state", ct) for ct in range(4)], w=[("state", ct) for ct in range(4)])
            s.barrier()

            for bi, (mode, _, b) in enumerate(blocks):
                nx = bi + 1 if bi + 1 < nblk else None
                if mode != "main":
                    continue
                def ah(h, nx=nx):
                    if nx is not None:
                        A_pre(nx, 2 * h); A_pre(nx, 2 * h + 1)
                sm = {0: 0, 1: 1, 2: 0, 3: 1}
                stageB(mode, b)
                for tt in range(4):
                    if tt >= 1:
                        TAIL(b, tt - 1)
                    ATT(b, tt)
                    if tt < 2:
                        lru_back([2 + tt], sm, True)
                        ah(tt)
                    if tt == 1:
                        lru_norm()
                    if tt >= 2:
                        if nx is not None:
                            A_tr(nx, 2 * (tt - 2)); A_tr(nx, 2 * (tt - 2) + 1)
                        TAIL_PE(b, tt - 2)
                TAIL(b, 3)
                TAIL_PE(b, 2)
                TAIL_PE(b, 3)

        if 2 in phases:
            phase12(srcs[2], dsts[2])
            s.barrier()
        if 0 in phases:
            phase0()
            s.barrier()
        if 3 in phases:
            phase3(srcs[3], dsts[3])
            s.barrier()
        if 4 in phases:
            phase4(srcs[4], dsts[4])

        s.barrier()
        s.op("sp", None)
        s.emit({"pe": block.tensor, "act": block.scalar, "dve": block.vector, "pool": block.gpsimd, "sp": block.sync}, sems)
    return nc, s


def _host_inputs(x, mem, g_mix, w_in, rel_bias, conv_w, conv_b, w_rg, b_rg, w_ig, b_ig, lru_L,
                 g_out_attn, g_out_lru, w_out, g_cross, g_mem, wq_c, wk_c, wv_c, wo_c,
                 g_ffn, w_gate, w_up, w_down, g_final):
    f32 = np.float32
    x = np.asarray(x, f32); mem = np.asarray(mem, f32)
    l = 0

    def col8(v):
        return np.asarray(v, f32).reshape(-1, 128).T

    shared = {}
    cols = np.zeros((128, NCOL), f32)
    cols[:, C_GMIX:C_GMIX + 8] = col8(g_mix[l]); cols[:, C_GCROSS:C_GCROSS + 8] = col8(g_cross[l])
    cols[:, C_GFFN:C_GFFN + 8] = col8(g_ffn[l]); cols[:, C_GMEM:C_GMEM + 8] = col8(g_mem[l])
    cols[:, C_GLRU:C_GLRU + 4] = col8(g_out_lru[l])
    cw = np.asarray(conv_w[l], f32)
    for j in range(4):
        cols[:, C_CONVW + 4 * j:C_CONVW + 4 * j + 4] = col8(cw[j])
    cols[:, C_CONVB:C_CONVB + 4] = col8(conv_b[l]); cols[:, C_BRG:C_BRG + 4] = col8(b_rg[l])
    cols[:, C_BIG:C_BIG + 4] = col8(b_ig[l]); cols[:, C_L:C_L + 4] = col8(lru_L[l])
    kk = np.arange(128)[:, None]; qq = np.arange(128)[None, :]
    bt = np.empty((128, 4, 5, 2, 128), f32)
    rb = np.asarray(rel_bias[l], f32)
    for r in range(5):
        rel = (4 - r) * 128 + qq - kk
        dc = 8 - 2 * r + qq // 64 - kk // 64
        idx = np.clip(rel, -128, 128) + 128
        vis = (dc >= 0) & (dc <= 8)
        for h in range(8):
            bt[:, h // 2, r, h % 2, :] = np.where(vis, rb[h][idx], f32(NEG))
    wrg = np.zeros((4, 128, 128), f32); wig = np.zeros((4, 128, 128), f32)
    for ct in range(4):
        for q in range(2):
            wrg[ct, q * 64:(q + 1) * 64, q * 64:(q + 1) * 64] = np.asarray(w_rg[l][2 * ct + q], f32)
            wig[ct, q * 64:(q + 1) * 64, q * 64:(q + 1) * 64] = np.asarray(w_ig[l][2 * ct + q], f32)
    shared.update({
        "g_final": np.asarray(g_final, f32), "g_out_attn": np.asarray(g_out_attn[l], f32),
        "g_mix": np.asarray(g_mix[l], f32), "g_cross": np.asarray(g_cross[l], f32), "g_ffn": np.asarray(g_ffn[l], f32), "g_mem": np.asarray(g_mem[l], f32),
        "ident": np.eye(128, dtype=f32), "bias_t": bt.reshape(128, -1),
        "w_in": np.asarray(w_in[l], f32), "w_out": np.asarray(w_out[l], f32), "wbd_rg": wrg, "wbd_ig": wig,
        "wq_c": np.asarray(wq_c[l], f32), "wk_c": np.asarray(wk_c[l], f32), "wv_c": np.asarray(wv_c[l], f32),
        "wo_c": np.asarray(wo_c[l], f32), "w_gate": np.asarray(w_gate[l], f32), "w_up": np.asarray(w_up[l], f32),
        "w_down": np.asarray(w_down[l], f32),
    })
    in_maps = []
    zeros = np.zeros((T, D), f32)
    for c in range(8):
        b, hf = c // 2, c % 2
        cc = cols.copy(); cc[:, C_FLAG] = float(hf)
        m = dict(shared)
        m["x_main"] = np.ascontiguousarray(x[b, hf * T:(hf + 1) * T])
        m["x_prev"] = np.ascontiguousarray(x[b, 0:T]) if hf == 1 else zeros
        m["mem"] = np.ascontiguousarray(mem[b])
        m["cols"] = cc
        in_maps.append(m)
    return in_maps


_CACHE = {}


def kernel(**inputs):
    in_maps = _host_inputs(**inputs)
    if "nc" not in _CACHE:
        _CACHE["nc"] = build()[0]
    res = run_bass_kernel_spmd(_CACHE["nc"], in_maps, core_ids=list(range(8)))
    out = np.empty((4, 2 * T, D), np.float32)
    for c in range(8):
        out[c // 2, (c % 2) * T:(c % 2 + 1) * T] = res.results[c]["out"]
    return out
```

```python
import contextlib
import numpy as np
import concourse.bass as bass
import concourse.mybir as mybir
from concourse.bass_utils import run_bass_kernel_spmd

F32 = mybir.dt.float32
BF16 = mybir.dt.bfloat16
AF = mybir.ActivationFunctionType
ALU = mybir.AluOpType

D = 1024
T = 4096
BT = 512
NBLK = T // BT
DFF = 2816
NF = DFF // 128
M = 256
EPS = 1e-6
NCOL = 72
NEG = -30000.0

C_GMIX, C_GCROSS, C_GFFN, C_GMEM, C_GLRU, C_CONVW, C_CONVB, C_BRG, C_BIG, C_L, C_FLAG = 0, 8, 16, 24, 32, 36, 52, 56, 60, 64, 68


class _Op:
    __slots__ = ("idx", "eng", "fn", "deps", "is_dma", "ndma", "stream", "sigval", "cum")

    def __init__(self, idx, eng, fn, deps, is_dma=False, ndma=0, stream=None):
        self.idx = idx; self.eng = eng; self.fn = fn; self.deps = deps
        self.is_dma = is_dma; self.ndma = ndma; self.stream = stream
        self.sigval = None; self.cum = None


class Sched:
    def __init__(self):
        self.ops = []
        self.lw = {}
        self.rd = {}
        self.dma_cum = {}

    def _deps(self, r, w):
        deps = {}
        for k in r:
            i = self.lw.get(k)
            if i is not None:
                deps[i] = True
        for k in w:
            i = self.lw.get(k)
            if i is not None:
                deps.setdefault(i, False)
            for j in self.rd.get(k, {}).values():
                deps.setdefault(j, False)
        return deps

    def _commit(self, idx, who, r, w):
        for k in r:
            self.rd.setdefault(k, {})[who] = idx
        for k in w:
            self.lw[k] = idx
            self.rd[k] = {}

    def op(self, eng, fn, r=(), w=(), x=()):
        r = tuple(r) + ("ARENA",); w = tuple(w) + tuple(x)
        idx = len(self.ops)
        deps = self._deps(r, w)
        for k in x:
            i = self.lw.get(k)
            if i is not None:
                deps[i] = True
        self.ops.append(_Op(idx, eng, fn, deps))
        self._commit(idx, eng, r, w)
        return idx

    def dma(self, q, fn, n, stream, r=(), w=()):
        r = tuple(r) + ("ARENA",); w = tuple(w)
        idx = len(self.ops)
        deps = self._deps(r, w)
        o = _Op(idx, q, fn, deps, True, n, stream)
        self.dma_cum[stream] = self.dma_cum.get(stream, 0) + n
        o.cum = self.dma_cum[stream]
        self.ops.append(o)
        self._commit(idx, ("dma", stream), r, w)
        return idx

    def barrier(self):
        idx = len(self.ops)
        deps = self._deps((), ("ARENA",))
        self.ops.append(_Op(idx, "sp", lambda e: e.nop(), deps))
        self._commit(idx, "sp", (), ("ARENA",))

    def emit(self, block_engines, sems):
        ops = self.ops

        def skip(o, p, raw):
            return (not o.is_dma) and (not p.is_dma) and p.eng == o.eng and (o.eng == "pe" or not raw)

        needed = [False] * len(ops)
        for o in ops:
            for i, raw in o.deps.items():
                p = ops[i]
                if p.is_dma or skip(o, p, raw):
                    continue
                needed[i] = True
        cnt = {}
        for o in ops:
            if (not o.is_dma) and needed[o.idx]:
                cnt[o.eng] = cnt.get(o.eng, 0) + 1
                o.sigval = cnt[o.eng]
        self.stats = dict(cnt)
        per_eng = {}
        for o in ops:
            per_eng.setdefault(o.eng, []).append(o)

        def run(engname, e):
            seen = {}
            for o in per_eng.get(engname, ()):
                waits = {}
                for i, raw in o.deps.items():
                    p = ops[i]
                    if p.is_dma:
                        key = p.stream; val = 16 * p.cum
                    else:
                        if skip(o, p, raw):
                            continue
                        key = p.eng; val = p.sigval
                    if val > waits.get(key, 0):
                        waits[key] = val
                for key, val in waits.items():
                    if val > seen.get(key, 0):
                        seen[key] = val
                        e.wait_ge(sems[key], val)
                if o.fn is None:
                    continue
                ins = o.fn(e)
                if o.is_dma:
                    assert len(ins) == o.ndma, (len(ins), o.ndma)
                    for xx in ins:
                        xx.then_inc(sems[o.stream], 16)
                elif o.sigval is not None:
                    ins.then_inc(sems[o.eng], 1)

        for engname, reg in block_engines.items():
            reg(lambda e, _n=engname: run(_n, e))


class Arena:
    def __init__(self, ap, nwords):
        self.ap = ap; self.n = nwords; self.off = 0

    def alloc_top(self, free_shape, dt):
        n = int(np.prod(free_shape))
        words = n if dt == F32 else (n + 1) // 2
        words = (words + 3) // 4 * 4
        self.top = getattr(self, "top", self.n) - words
        save = self.off
        self.off = self.top
        a = self.alloc(free_shape, dt)
        self.off = save
        return a

    def alloc(self, free_shape, dt):
        n = int(np.prod(free_shape))
        words = n if dt == F32 else (n + 1) // 2
        words = (words + 3) // 4 * 4
        a = self.ap[:, self.off:self.off + words]
        self.off += words
        assert self.off <= self.n, ("arena overflow", self.off, self.n)
        if dt != F32:
            a = a.bitcast(dt)
        a = a[:, 0:n]
        if len(free_shape) == 2:
            a = a.rearrange("p (a b) -> p a b", a=free_shape[0])
        elif len(free_shape) == 3:
            a = a.rearrange("p (a b c) -> p a b c", a=free_shape[0], b=free_shape[1])
        return a


def build(phases=(0, 1, 2, 3, 4)):
    nc = bass.Bass("TRN2", target_bir_lowering=False)

    def din(name, shape):
        return nc.dram_tensor(name, list(shape), F32, kind="ExternalInput").ap()

    x_main = din("x_main", [T, D]); x_prev = din("x_prev", [T, D]); mem_d = din("mem", [M, D])
    cols_d = din("cols", [128, NCOL]); gfin_d = din("g_final", [D]); gatt_d = din("g_out_attn", [512])
    gmix_d = din("g_mix", [D]); gcr_d = din("g_cross", [D]); gffn_d = din("g_ffn", [D]); gmem_d = din("g_mem", [D])
    ident_d = din("ident", [128, 128]); bt_d = din("bias_t", [128, 8 * 5 * 128])
    w_in_d = din("w_in", [D, 2560]); w_out_d = din("w_out", [D, D])
    wrg_d = din("wbd_rg", [4, 128, 128]); wig_d = din("wbd_ig", [4, 128, 128])
    wq_d = din("wq_c", [D, D]); wk_d = din("wk_c", [D, D]); wv_d = din("wv_c", [D, D]); wo_d = din("wo_c", [D, D])
    wg_d = din("w_gate", [D, DFF]); wu_d = din("w_up", [D, DFF]); wd_d = din("w_down", [DFF, D])
    out_d = nc.dram_tensor("out", [T, D], F32, kind="ExternalOutput").ap()
    scr = [nc.dram_tensor(f"scr{i}", [T, D], F32, kind="Internal").ap() for i in range(2)]

    main = [p for p in phases if p in (2, 3, 4)]
    srcs, dsts = {}, {}
    cur = x_main
    for i, p in enumerate(main):
        srcs[p] = cur
        dsts[p] = out_d if i == len(main) - 1 else scr[i % 2]
        cur = dsts[p]

    s = Sched()
    with contextlib.ExitStack() as es:
        NW = 212000 // 4
        arena_t = es.enter_context(nc.sbuf_tensor("arena", [128, NW], F32))
        AR = Arena(arena_t[:, :], NW)
        ps_names = ["T", "X", "P0", "P1", "S0", "S1", "O0", "O1"]
        PSALL = es.enter_context(nc.psum_tensor("psall", [128, 4096], F32))
        PS = {n: PSALL[:, i * 512:(i + 1) * 512] for i, n in enumerate(ps_names)}
        PSP = {"B": PSALL[:, 1024:2048], "C": PSALL[:, 2048:3072], "D": PSALL[:, 3072:4096]}
        class _Sems(dict):
            def __missing__(self, name):
                v = es.enter_context(nc.semaphore(str(name)))
                self[name] = v
                return v
        sems = _Sems()
        block = es.enter_context(nc.Block())

        ident = AR.alloc([128], BF16)
        ones = AR.alloc([128], BF16)
        cols = AR.alloc([NCOL], F32)
        c1 = AR.alloc([4], F32); c2 = AR.alloc([4], F32); epsb = AR.alloc([1], F32); tmp4 = AR.alloc([4], F32)
        mhalf = AR.alloc([1], F32); phalf = AR.alloc([1], F32); negb = AR.alloc([8], F32)
        PERS0 = AR.off
        KxT = AR.alloc([8, M], BF16)
        Vx = AR.alloc([2, D], BF16)
        PERS1 = AR.off

        s.dma("sp", lambda e: [e.dma_start(out=cols, in_=cols_d)], 1, "c_cols", w=["cols"])
        s.dma("pool", lambda e: [e.dma_start(out=ident, in_=ident_d)], 1, "c_ident", w=["ident"])
        s.op("dve", lambda e: e.memset(ones, 1.0), w=["ones"])
        s.op("dve", lambda e: e.memset(epsb, EPS), w=["epsb"])
        s.op("dve", lambda e: e.memset(mhalf, -0.5), w=["mhalf"])
        s.op("dve", lambda e: e.memset(phalf, 0.5), w=["phalf"])
        s.op("dve", lambda e: e.tensor_scalar(out=negb, in0=cols[:, C_BRG:C_BRG + 8], scalar1=0.5, scalar2=0.0, op0=ALU.mult, op1=ALU.add), r=["cols"], w=["negb"])
        s.op("act", lambda e: e.activation(out=tmp4, in_=cols[:, C_L:C_L + 4], func=AF.Exp, scale=-1.0), r=["cols"], w=["tmp4"])
        s.op("act", lambda e: e.activation(out=tmp4, in_=tmp4, func=AF.Ln, bias=1.0, scale=1.0), r=["tmp4"], w=["tmp4"])
        s.op("dve", lambda e: e.tensor_scalar(out=c1, in0=tmp4, scalar1=-8.0, scalar2=None, op0=ALU.mult), r=["tmp4"], w=["c1"])
        s.op("dve", lambda e: e.tensor_scalar(out=c2, in0=tmp4, scalar1=-4.0, scalar2=None, op0=ALU.mult), r=["tmp4"], w=["c2"])

        def load_w(dst, src_d, nk, key, c0=None, c1_=None):
            def fn(e):
                res = []
                for kt in range(nk):
                    src = src_d[kt * 128:(kt + 1) * 128, :] if c0 is None else src_d[kt * 128:(kt + 1) * 128, c0:c1_]
                    res.append(e.dma_start(out=dst[:, kt, :], in_=src))
                return res
            s.dma("pool", fn, nk, "w_" + key, w=[key])

        def load_row(dst, src_d, key):
            s.dma("sp", lambda e: [e.dma_start(out=dst, in_=src_d.partition_broadcast(128))], 1, "c_" + key, w=[key])

        def rstd_pool(ss, inv_n, tmp, rstd, kss, kout):
            s.op("act", lambda e: e.activation(out=tmp, in_=ss, func=AF.Sqrt, bias=epsb, scale=inv_n), r=[kss, "epsb"], w=[(kout, "d")])
            s.op("dve", lambda e: e.reciprocal(out=rstd, in_=tmp), r=[(kout, "d")], w=[kout])

        class NormBufs:
            def __init__(self, tag, grow, gkey, nxn=4):
                self.tag = tag; self.grow = grow; self.gkey = gkey
                self.xs = [AR.alloc([D], F32) for _ in range(2)]
                self.xn = [AR.alloc([D], BF16) for _ in range(nxn)]
                self.st = [[AR.alloc([1], F32) for _ in range(3)] for _ in range(2)]
                self.n = 0

        def norm_pre(nb, xsrc_ap):
            n = nb.n; nb.n += 1
            a = n % 2; j = n % len(nb.xn)
            xs = nb.xs[a]; xn = nb.xn[j]; ss, sd, rstd = nb.st[a]
            kx = f"xs{nb.tag}{a}"; kn = f"xn{nb.tag}{j}"; kst = f"st{nb.tag}{a}"
            s.dma("sp", lambda e: [e.dma_start(out=xs, in_=xsrc_ap)], 1, "ld" + kx, w=[kx])
            s.op("act", lambda e: e.activation(out=xn, in_=xs, func=AF.Square, accum_out=ss), r=[kx], w=[kn, kst + "s"])
            rstd_pool(ss, 1.0 / D, sd, rstd, kst + "s", kst + "r")
            s.op("dve", lambda e: e.scalar_tensor_tensor(out=xn, in0=xs, scalar=rstd, in1=nb.grow, op0=ALU.mult, op1=ALU.mult),
                 r=[kx, kst + "r", nb.gkey], w=[kn])
            return j

        def norm_tr(nb, j, hT, col0, tbank, evac_eng):
            xn = nb.xn[j]; kn = f"xn{nb.tag}{j}"
            tp = PS[tbank][:, :].bitcast(BF16)

            def tr(e):
                ins = None
                for kt in range(8):
                    ins = e.transpose(tp[:, kt * 128:(kt + 1) * 128], xn[:, kt * 128:(kt + 1) * 128], ident)
                return ins
            s.op("pe", tr, r=[kn, "ident"], x=[tbank])
            tp3 = tp.rearrange("p (a b) -> p a b", a=8)
            if evac_eng == "act":
                s.op("act", lambda e: e.activation(out=hT[:, :, col0:col0 + 128], in_=tp3, func=AF.Copy), x=[tbank], w=[("hT", id(hT), col0)])
            else:
                s.op("dve", lambda e: e.tensor_copy(out=hT[:, :, col0:col0 + 128], in_=tp3), x=[tbank], w=[("hT", id(hT), col0)])

        def hT_keys(hT, ncol=BT):
            return [("hT", id(hT), c) for c in range(0, ncol, 128)]

        TB = ["T", "X"]

        Wg = AR.alloc_top([8, DFF], BF16); Wu = AR.alloc_top([8, DFF], BF16)
        WTOP = AR.top
        ffn_w_loaded = [False]

        def load_ffn_w():
            if not ffn_w_loaded[0]:
                load_w(Wg, wg_d, 8, "Wg"); load_w(Wu, wu_d, 8, "Wu")
                ffn_w_loaded[0] = True

        def phase4(xsrc, xdst):
            AR.off = PERS0
            Wd = AR.alloc([NF, D], BF16)
            gffn = AR.alloc([D], F32); gfin = AR.alloc([D], F32)
            nb = NormBufs("f", gffn, "gffn")
            xr = [AR.alloc([D], F32) for _ in range(2)]
            junk = AR.alloc([D], BF16)
            hT = AR.alloc([8, BT], BF16)
            aT = AR.alloc([NF, BT], BF16)
            sg = [AR.alloc([BT], F32) for _ in range(2)]
            fst = [[AR.alloc([1], F32) for _ in range(3)] for _ in range(2)]
            print("arena words used (phase4):", AR.off, "of", AR.n)
            load_row(gffn, gffn_d, "gffn"); load_row(gfin, gfin_d, "gfin")
            assert AR.off <= WTOP, (AR.off, WTOP)
            load_ffn_w(); load_w(Wd, wd_d, NF, "Wd")
            hk = hT_keys(hT)
            slots = {}

            def A_pre(b, tt):
                r0 = b * BT + tt * 128
                slots[(b, tt)] = norm_pre(nb, xsrc[r0:r0 + 128, :])

            def A_tr(b):
                for tt in range(4):
                    norm_tr(nb, slots[(b, tt)], hT, tt * 128, TB[tt % 2], "act")

            def GU(b):
                for f in range(NF):
                    pg = "P0" if f % 2 == 0 else "S0"
                    pu = "P1" if f % 2 == 0 else "S1"

                    def mm(e, W, bank, f=f):
                        ins = None
                        for kt in range(8):
                            ins = e.matmul(PS[bank][:, :], lhsT=W[:, kt, f * 128:(f + 1) * 128], rhs=hT[:, kt, :], start=(kt == 0), stop=(kt == 7))
                        return ins
                    s.op("pe", lambda e, f=f, pg=pg, mm=mm: mm(e, Wg, pg), r=["Wg"] + hk, x=[pg])
                    s.op("pe", lambda e, f=f, pu=pu, mm=mm: mm(e, Wu, pu), r=["Wu"] + hk, x=[pu])
                    sgi = sg[f % 2]
                    s.op("act", lambda e, pg=pg, sgi=sgi: e.activation(out=sgi, in_=PS[pg][:, :], func=AF.Silu), x=[pg], w=[("sg", f % 2)])
                    s.op("dve", lambda e, pu=pu, sgi=sgi, f=f: e.tensor_tensor(out=aT[:, f, :], in0=PS[pu][:, :], in1=sgi, op=ALU.mult),
                         r=[("sg", f % 2)], x=[pu], w=[("aT", f)])
                    if b + 1 < NBLK and f in (3, 8, 13, 18):
                        A_pre(b + 1, (f - 3) // 5)

            def DN(b):
                for tt in range(4):
                    r0 = b * BT + tt * 128
                    i = tt % 2
                    s.dma("sp", lambda e, i=i, r0=r0: [e.dma_start(out=xr[i], in_=xsrc[r0:r0 + 128, :])], 1, f"ldr{i}", w=[("xr", i)])
                    for hf in range(2):
                        bank = "O0" if hf == 0 else "O1"

                        def dm(e, tt=tt, hf=hf, bank=bank):
                            ins = None
                            for f in range(NF):
                                ins = e.matmul(PS[bank][:, :], lhsT=aT[:, f, tt * 128:(tt + 1) * 128], rhs=Wd[:, f, hf * 512:(hf + 1) * 512],
                                               start=(f == 0), stop=(f == NF - 1))
                            return ins
                        s.op("pe", dm, r=["Wd"] + [("aT", f) for f in range(NF)], x=[bank])
                        s.op("dve", lambda e, i=i, hf=hf, bank=bank: e.tensor_tensor(out=xr[i][:, hf * 512:(hf + 1) * 512], in0=PS[bank][:, :],
                                                                                   in1=xr[i][:, hf * 512:(hf + 1) * 512], op=ALU.add),
                             r=[("xr", i)], x=[bank], w=[("xr", i)])
                    ss, sd, rstd = fst[i]
                    s.op("act", lambda e, i=i, ss=ss: e.activation(out=junk, in_=xr[i], func=AF.Square, accum_out=ss), r=[("xr", i)], w=["junk", ("fs", i)])
                    rstd_pool(ss, 1.0 / D, sd, rstd, ("fs", i), ("fr", i))
                    s.op("dve", lambda e, i=i, rstd=rstd: e.scalar_tensor_tensor(out=xr[i], in0=xr[i], scalar=rstd, in1=gfin, op0=ALU.mult, op1=ALU.mult),
                         r=[("xr", i), ("fr", i), "gfin"], w=[("xr", i)])
                    s.dma("pool", lambda e, i=i, r0=r0: [e.dma_start(out=xdst[r0:r0 + 128, :], in_=xr[i])], 1, f"st{i}", r=[("xr", i)], w=[("dst4", r0)])

            for tt in range(4):
                A_pre(0, tt)
            A_tr(0)
            for b in range(NBLK):
                GU(b)
                if b + 1 < NBLK:
                    A_tr(b + 1)
                DN(b)

        def phase0():
            AR.off = PERS1
            Wk = AR.alloc([8, D], BF16); Wv = AR.alloc([8, D], BF16)
            gmem = AR.alloc([D], F32)
            nb = NormBufs("m", gmem, "gmem", nxn=2)
            hTm = AR.alloc([8, M], BF16)
            load_row(gmem, gmem_d, "gmem")
            load_w(Wk, wk_d, 8, "Wk"); load_w(Wv, wv_d, 8, "Wv")
            for mt in range(2):
                j = norm_pre(nb, mem_d[mt * 128:(mt + 1) * 128, :])
                norm_tr(nb, j, hTm, mt * 128, TB[mt], "act")
            hk = hT_keys(hTm, M)
            for f in range(8):
                bank = "P0" if f % 2 == 0 else "P1"

                def mmk(e, f=f, bank=bank):
                    ins = None
                    for kt in range(8):
                        ins = e.matmul(PS[bank][:, 0:M], lhsT=Wk[:, kt, f * 128:(f + 1) * 128], rhs=hTm[:, kt, :], start=(kt == 0), stop=(kt == 7))
                    return ins
                s.op("pe", mmk, r=["Wk"] + hk, x=[bank])
                s.op("act", lambda e, f=f, bank=bank: e.activation(out=KxT[:, f, :], in_=PS[bank][:, 0:M], func=AF.Copy), x=[bank], w=["KxT"])
            for mt in range(2):
                for hf in range(2):
                    bank = "S0" if hf == 0 else "S1"

                    def mmv(e, mt=mt, hf=hf, bank=bank):
                        ins = None
                        for kt in range(8):
                            ins = e.matmul(PS[bank][:, :], lhsT=hTm[:, kt, mt * 128:(mt + 1) * 128], rhs=Wv[:, kt, hf * 512:(hf + 1) * 512], start=(kt == 0), stop=(kt == 7))
                        return ins
                    s.op("pe", mmv, r=["Wv"] + hk, x=[bank])
                    s.op("dve", lambda e, mt=mt, hf=hf, bank=bank: e.tensor_copy(out=Vx[:, mt, hf * 512:(hf + 1) * 512], in_=PS[bank][:, :]), x=[bank], w=["Vx"])

        def phase3(xsrc, xdst):
            AR.off = PERS1
            Wq = AR.alloc([8, D], BF16); Wo = AR.alloc([8, D], BF16)
            gcr = AR.alloc([D], F32)
            nb = NormBufs("c", gcr, "gcr")
            xr = [AR.alloc([D], F32) for _ in range(4)]
            hT = AR.alloc([8, BT], BF16)
            qT = AR.alloc([8, BT], BF16)
            PT = [[AR.alloc([BT], BF16) for _ in range(2)] for _ in range(2)]
            rden = [AR.alloc([BT], F32) for _ in range(2)]
            oxT = AR.alloc([8, BT], BF16)
            load_row(gcr, gcr_d, "gcr")
            load_w(Wq, wq_d, 8, "Wq"); load_w(Wo, wo_d, 8, "Wo")
            assert AR.off <= WTOP, (AR.off, WTOP)
            load_ffn_w()
            hk = hT_keys(hT)
            slots = {}

            def A_pre(b):
                for tt in range(4):
                    r0 = b * BT + tt * 128
                    slots[(b, tt)] = norm_pre(nb, xsrc[r0:r0 + 128, :])

            def A_tr(b):
                for tt in range(4):
                    norm_tr(nb, slots[(b, tt)], hT, tt * 128, TB[tt % 2], "dve")

            def Q(b):
                for f in range(8):
                    bank = "P0" if f % 2 == 0 else "P1"

                    def mq(e, f=f, bank=bank):
                        ins = None
                        for kt in range(8):
                            ins = e.matmul(PS[bank][:, :], lhsT=Wq[:, kt, f * 128:(f + 1) * 128], rhs=hT[:, kt, :], start=(kt == 0), stop=(kt == 7))
                        return ins
                    s.op("pe", mq, r=["Wq"] + hk, x=[bank])
                    s.op("act", lambda e, f=f, bank=bank: e.activation(out=qT[:, f, :], in_=PS[bank][:, :], func=AF.Copy), x=[bank], w=[("qT", f)])

            def ATT(b):
                def SC(h):
                    hp = h % 2
                    for mt in range(2):
                        bank = ("S0", "S1")[mt] if h % 2 == 0 else ("P0", "P1")[mt]

                        def msc(e, h=h, mt=mt, bank=bank):
                            ins = None
                            for dt in range(2):
                                ins = e.matmul(PS[bank][:, :], lhsT=KxT[:, 2 * h + dt, mt * 128:(mt + 1) * 128], rhs=qT[:, 2 * h + dt, :], start=(dt == 0), stop=(dt == 1))
                            return ins
                        s.op("pe", msc, r=["KxT", ("qT", 2 * h), ("qT", 2 * h + 1)], x=[bank])
                        s.op("act", lambda e, hp=hp, mt=mt, bank=bank: e.activation(out=PT[hp][mt], in_=PS[bank][:, :], func=AF.Exp, scale=1.0 / 16.0),
                             x=[bank], w=[("PT", hp, mt)])

                def DPV(h):
                    hp = h % 2

                    def mden(e, hp=hp):
                        ins = None
                        for mt in range(2):
                            ins = e.matmul(PS["O0"][:, :], lhsT=ones, rhs=PT[hp][mt], start=(mt == 0), stop=(mt == 1))
                        return ins
                    s.op("pe", mden, r=["ones", ("PT", hp, 0), ("PT", hp, 1)], x=["O0"])
                    s.op("dve", lambda e, hp=hp: e.reciprocal(out=rden[hp], in_=PS["O0"][:, :]), x=["O0"], w=[("rden", hp)])
                    for dt in range(2):
                        bank = "O1" if dt == 0 else "X"
                        f = 2 * h + dt

                        def mpv(e, f=f, hp=hp, bank=bank):
                            ins = None
                            for mt in range(2):
                                ins = e.matmul(PS[bank][:, :], lhsT=Vx[:, mt, f * 128:(f + 1) * 128], rhs=PT[hp][mt], start=(mt == 0), stop=(mt == 1))
                            return ins
                        s.op("pe", mpv, r=["Vx", ("PT", hp, 0), ("PT", hp, 1)], x=[bank])
                        s.op("dve", lambda e, f=f, hp=hp, bank=bank: e.tensor_tensor(out=oxT[:, f, :], in0=PS[bank][:, :], in1=rden[hp], op=ALU.mult),
                             r=[("rden", hp)], x=[bank], w=[("oxT", f)])

                SC(0)
                for h in range(4):
                    if h + 1 < 4:
                        SC(h + 1)
                    DPV(h)

            def XR(b):
                for tt in range(4):
                    r0 = b * BT + tt * 128
                    s.dma("sp", lambda e, i=tt, r0=r0: [e.dma_start(out=xr[i], in_=xsrc[r0:r0 + 128, :])], 1, f"ldr{tt}", w=[("xr", tt)])

            def OUT(b):
                for tt in range(4):
                    r0 = b * BT + tt * 128
                    i = tt
                    for hf in range(2):
                        bank = "P0" if hf == 0 else "P1"

                        def mo(e, tt=tt, hf=hf, bank=bank):
                            ins = None
                            for f in range(8):
                                ins = e.matmul(PS[bank][:, :], lhsT=oxT[:, f, tt * 128:(tt + 1) * 128], rhs=Wo[:, f, hf * 512:(hf + 1) * 512], start=(f == 0), stop=(f == 7))
                            return ins
                        s.op("pe", mo, r=["Wo"] + [("oxT", f) for f in range(8)], x=[bank])
                        s.op("dve", lambda e, i=i, hf=hf, bank=bank: e.tensor_tensor(out=xr[i][:, hf * 512:(hf + 1) * 512], in0=PS[bank][:, :],
                                                                                   in1=xr[i][:, hf * 512:(hf + 1) * 512], op=ALU.add),
                             r=[("xr", i)], x=[bank], w=[("xr", i)])
                    s.dma("pool", lambda e, i=i, r0=r0: [e.dma_start(out=xdst[r0:r0 + 128, :], in_=xr[i])], 1, f"st{i}", r=[("xr", i)], w=[("dst3", r0)])

            A_pre(0); A_tr(0)
            for b in range(NBLK):
                if b + 1 < NBLK:
                    A_pre(b + 1)
                Q(b)
                if b + 1 < NBLK:
                    A_tr(b + 1)
                XR(b)
                ATT(b)
                OUT(b)

        def phase12(xsrc, xdst):
            AR.off = PERS0
            Win = AR.alloc([8, 2560], BF16); Wout = AR.alloc([8, D], BF16)
            Wrg = AR.alloc([4, 128], BF16); Wig = AR.alloc([4, 128], BF16)
            EB = AR.alloc([4, 5 * 256], BF16)
            gmix = AR.alloc([D], F32); gatt = AR.alloc([512], F32)
            nb = NormBufs("x", gmix, "gmix")
            xr = [AR.alloc([D], F32) for _ in range(2)]
            hT = AR.alloc([8, BT], BF16)
            QP = AR.alloc([4, 2, BT], BF16)
            KT = AR.alloc([4, 1024], BF16)
            VA = AR.alloc([8, 8, 65], BF16)
            XU = AR.alloc([4, 3 + BT], BF16)
            DW = AR.alloc([16, 128], BF16)
            L_u = [AR.alloc([BT], F32) for _ in range(2)]
            L_ub = [AR.alloc([BT], BF16) for _ in range(2)]
            L_r = [AR.alloc([BT], F32) for _ in range(2)]
            L_i = [AR.alloc([BT], F32) for _ in range(2)]
            L_a = [AR.alloc([BT], F32) for _ in range(2)]
            L_h = [AR.alloc([BT], F32) for _ in range(2)]
            state = AR.alloc([4], F32)
            mo0 = AR.off
            gg = AR.alloc([4, BT], F32)
            rec = AR.alloc([4, BT], F32)
            att = [AR.alloc([8, 64], F32) for _ in range(4)]
            mo1 = AR.off
            AR.off = mo0
            for _ in range(2):
                L_u.append(AR.alloc([BT], F32)); L_ub.append(AR.alloc([BT], BF16)); L_r.append(AR.alloc([BT], F32))
                L_i.append(AR.alloc([BT], F32)); L_a.append(AR.alloc([BT], F32)); L_h.append(AR.alloc([BT], F32))
            assert AR.off <= mo1, (AR.off, mo1)
            AR.off = mo1
            sq = AR.alloc([4, BT], BF16)
            recb = AR.alloc([4, BT], BF16)
            rstdB = AR.alloc([BT], F32)
            PTp = [AR.alloc([5 * 256], BF16) for _ in range(2)]
            rc = [AR.alloc([8], F32) for _ in range(4)]
            attb = [AR.alloc([512], BF16) for _ in range(2)]
            attT = [AR.alloc([4, 128], BF16) for _ in range(2)]
            ast = [[AR.alloc([1], F32) for _ in range(3)] for _ in range(2)]
            print("arena words used (phase12):", AR.off, "of", AR.n)

            load_row(gmix, gmix_d, "gmix"); load_row(gatt, gatt_d, "gatt")
            load_w(Win, w_in_d, 8, "Win")
            s.dma("pool", lambda e: [e.dma_start(out=Wrg[:, ct, :], in_=wrg_d[ct]) for ct in range(4)] +
                                    [e.dma_start(out=Wig[:, ct, :], in_=wig_d[ct]) for ct in range(4)] +
                                    [e.dma_start(out=EB[:, p, :], in_=bt_d[:, p * 1280:(p + 1) * 1280]) for p in range(4)], 12, "w_small",
                  w=["Wrg", "Wig", "EB"])
            load_w(Wout, w_out_d, 8, "Wout")
            s.op("act", lambda e: e.activation(out=EB, in_=EB, func=AF.Exp), r=["EB"], w=["EB"])
            s.op("dve", lambda e: e.memset(XU[:, :, 0:3], 0.0), w=[("XUh", ct) for ct in range(4)])

            def mkdw(e):
                ins = None
                for k in range(16):
                    ins = e.tensor_scalar(out=DW[:, k, :], in0=ident, scalar1=cols[:, C_CONVW + k:C_CONVW + k + 1], scalar2=None, op0=ALU.mult)
                return ins
            s.op("dve", mkdw, r=["ident", "cols"], w=["DW"])
            s.op("dve", lambda e: e.memset(state, 0.0), w=[("state", ct) for ct in range(4)])
            s.op("dve", lambda e: e.memset(VA[:, :, :, 64:65], 1.0), w=[("VA", g) for g in range(8)])
            s.op("dve", lambda e: e.memset(QP, 0.0), w=[("QP", ft) for ft in range(4)])
            flag = cols[:, C_FLAG:C_FLAG + 1]
            hk = hT_keys(hT)
            slots = {}
            pbank = ["P0", "P1", "S0", "S1"]
            pc = [0]

            blocks = [("pre" if pb < NBLK - 1 else "prekv", x_prev[pb * BT:(pb + 1) * BT, :], -1) for pb in range(NBLK)]
            blocks += [("main", xsrc[b * BT:(b + 1) * BT, :], b) for b in range(NBLK)]

            def A_pre(bi, tt):
                slots[(bi, tt)] = norm_pre(nb, blocks[bi][1][tt * 128:(tt + 1) * 128, :])

            def A_tr(bi, tt):
                norm_tr(nb, slots[(bi, tt)], hT, tt * 128, TB[tt % 2], "act" if blocks[bi][0] != "main" else "dve")

            def proj_fm(col0, evac):
                bank = pbank[pc[0] % 4]; pc[0] += 1

                def mm(e, col0=col0, bank=bank):
                    ins = None
                    for kt in range(8):
                        ins = e.matmul(PS[bank][:, :], lhsT=Win[:, kt, col0:col0 + 128], rhs=hT[:, kt, :], start=(kt == 0), stop=(kt == 7))
                    return ins
                s.op("pe", mm, r=["Win"] + hk, x=[bank])
                evac(bank)

            def stageB(mode, b, after_half=lambda h: None):
                main_ = mode == "main"
                ringbase = ((4 * b) + 4) % 8 if mode != "pre" else 0

                def projK():
                    for ft in range(4):
                        proj_fm(512 + ft * 128, lambda bank, ft=ft: s.op(
                            "act", lambda e: e.activation(out=KT[:, ft, ringbase * 128:ringbase * 128 + BT], in_=PS[bank][:, :], func=AF.Copy),
                            x=[bank], w=[("KT", ringbase + t) for t in range(4)]))

                def projV():
                    for tt in range(4):
                        bank = pbank[pc[0] % 4]; pc[0] += 1
                        ring = ringbase + tt

                        def mv(e, tt=tt, bank=bank):
                            ins = None
                            for kt in range(8):
                                ins = e.matmul(PS[bank][:, :], lhsT=hT[:, kt, tt * 128:(tt + 1) * 128], rhs=Win[:, kt, 1024:1536], start=(kt == 0), stop=(kt == 7))
                            return ins
                        s.op("pe", mv, r=["Win"] + hk, x=[bank])
                        s.op("dve", lambda e, ring=ring, bank=bank: e.tensor_copy(out=VA[:, ring, :, 0:64], in_=PS[bank][:, :].rearrange("p (h d) -> p h d", h=8)),
                             x=[bank], w=[("VA", ring)])
                    if mode == "prekv":
                        s.op("dve", lambda e: e.tensor_scalar(out=VA[:, 0:4, :, 64:65], in0=VA[:, 0:4, :, 64:65], scalar1=flag, scalar2=None, op0=ALU.mult),
                             r=["cols"], w=[("VA", g) for g in range(4)])
                    if main_ and b == 1:
                        s.op("dve", lambda e: e.memset(VA[:, 0:4, :, 64:65], 1.0), w=[("VA", g) for g in range(4)])

                def projQ():
                    for ft in range(4):
                        def evq(bank, ft=ft):
                            def f(e):
                                e.activation(out=QP[0:64, ft, 0, :], in_=PS[bank][0:64, :], func=AF.Copy, scale=0.125)
                                return e.activation(out=QP[64:128, ft, 1, :], in_=PS[bank][64:128, :], func=AF.Copy, scale=0.125)
                            s.op("act", f, x=[bank], w=[("QP", ft)])
                        proj_fm(ft * 128, evq)

                def projXU():
                    for ct in range(4):
                        proj_fm(1536 + ct * 128, lambda bank, ct=ct: s.op(
                            "act", lambda e: e.activation(out=XU[:, ct, 3:3 + BT], in_=PS[bank][:, :], func=AF.Copy),
                            x=[bank], w=[("XU", ct)]))

                def projGU():
                    for ct in range(4):
                        proj_fm(2048 + ct * 128, lambda bank, ct=ct: s.op(
                            "act", lambda e: e.activation(out=gg[:, ct, :], in_=PS[bank][:, :], func=AF.Gelu_apprx_tanh),
                            x=[bank], w=[("gg", ct)]))

                if main_:
                    sm = {0: 0, 1: 1, 2: 0, 3: 1}
                    projXU(); projGU()
                    lru_front([0, 1], sm)
                    projK()
                    projV()
                    lru_back([0, 1], sm, True)
                    projQ()
                    lru_front([2, 3], sm)
                else:
                    projXU()
                    if mode == "prekv":
                        projK(); projV()

            GB = {0: ("T", "X"), 1: ("O0", "O1"), 2: ("S0", "S1"), 3: ("P0", "P1")}

            def lru_front(cts, smap):
                for ct in cts:
                    j_ = smap[ct]
                    u = L_u[j_]; ub = L_ub[j_]
                    br = GB[j_][0]
                    cb = cols[:, C_CONVB + ct:C_CONVB + ct + 1]

                    def cv(e, ct=ct, br=br):
                        ins = None
                        for j in range(4):
                            ins = e.matmul(PS[br][:, :], lhsT=DW[:, 4 * j + ct, :], rhs=XU[:, ct, j:j + BT], start=(j == 0), stop=(j == 3))
                        return ins
                    s.op("pe", cv, r=[("XU", ct), ("XUh", ct), "DW"], x=[br])
                    s.op("act", lambda e, u=u, br=br, cb=cb: e.activation(out=u, in_=PS[br][:, :], func=AF.Identity, bias=cb, scale=1.0),
                         r=["cols"], x=[br], w=[("u", j_)])
                    s.op("dve", lambda e, ub=ub, br=br, cb=cb: e.tensor_scalar(out=ub, in0=PS[br][:, :], scalar1=cb, scalar2=None, op0=ALU.add),
                         r=["cols"], x=[br], w=[("ub", j_)])
                    s.op("pool", lambda e, ct=ct: e.tensor_copy(out=XU[:, ct, 0:3], in_=XU[:, ct, BT:BT + 3]), r=[("XU", ct)], w=[("XUh", ct)])

            def lru_back(cts, smap, main_, mid=lambda: None):
                gb = GB
                for ct in cts:
                    j_ = smap[ct]
                    ub = L_ub[j_]
                    br, bi_ = gb[j_]
                    s.op("pe", lambda e, ct=ct, ub=ub, br=br: e.matmul(PS[br][:, :], lhsT=Wrg[:, ct, :], rhs=ub, start=True, stop=True),
                         r=["Wrg", ("ub", j_)], x=[br])
                    s.op("pe", lambda e, ct=ct, ub=ub, bi_=bi_: e.matmul(PS[bi_][:, :], lhsT=Wig[:, ct, :], rhs=ub, start=True, stop=True),
                         r=["Wig", ("ub", j_)], x=[bi_])
                for ct in cts:
                    j_ = smap[ct]
                    br, bi_ = gb[j_]
                    s.op("act", lambda e, ct=ct, j_=j_, br=br: e.activation(out=L_r[j_], in_=PS[br][:, :], func=AF.Tanh, bias=negb[:, ct:ct + 1], scale=0.5),
                         r=["negb"], x=[br], w=[("Lr", j_)])
                    s.op("act", lambda e, ct=ct, j_=j_, bi_=bi_: e.activation(out=L_i[j_], in_=PS[bi_][:, :], func=AF.Tanh, bias=negb[:, 4 + ct:5 + ct], scale=0.5),
                         r=["negb"], x=[bi_], w=[("Li", j_)])
                for ct in cts:
                    j_ = smap[ct]
                    for buf, key in ((L_i[j_], ("Li", j_)),):
                        s.op("dve", lambda e, buf=buf: e.tensor_scalar(out=buf, in0=buf, scalar1=0.5, scalar2=0.5, op0=ALU.mult, op1=ALU.add), r=[key], w=[key])
                for ct in cts:
                    j_ = smap[ct]
                    s.op("pool", lambda e, j_=j_: e.tensor_tensor(out=L_i[j_], in0=L_i[j_], in1=L_u[j_], op=ALU.mult),
                         r=[("Li", j_), ("u", j_)], w=[("Li", j_)])
                for ct in cts:
                    j_ = smap[ct]
                    s.op("act", lambda e, ct=ct, j_=j_: e.activation(out=L_a[j_], in_=L_r[j_], func=AF.Exp, scale=c2[:, ct:ct + 1], bias=c2[:, ct:ct + 1]),
                         r=[("Lr", j_), "c2"], w=[("La", j_)])
                    s.op("pool", lambda e, j_=j_: e.tensor_tensor(out=L_r[j_], in0=L_a[j_], in1=L_a[j_], op=ALU.mult), r=[("La", j_)], w=[("Lr", j_)])
                for ct in cts:
                    j_ = smap[ct]
                    s.op("act", lambda e, j_=j_: e.activation(out=L_r[j_], in_=L_r[j_], func=AF.Sqrt, bias=1.0, scale=-1.0),
                         r=[("Lr", j_)], w=[("Lr", j_)])
                mid()
                for ct in cts:
                    j_ = smap[ct]

                    s.op("pool", lambda e, j_=j_: e.tensor_tensor(out=L_i[j_], in0=L_i[j_], in1=L_r[j_], op=ALU.mult),
                         r=[("Li", j_), ("Lr", j_)], w=[("Li", j_)])
                    s.op("dve", lambda e, ct=ct, j_=j_: e.tensor_tensor_scan(out=L_h[j_], data0=L_a[j_], data1=L_i[j_], initial=state[:, ct:ct + 1],
                                                                          op0=ALU.mult, op1=ALU.add),
                         r=[("La", j_), ("Li", j_), ("state", ct)], w=[("Lh", j_)])
                    s.op("dve", lambda e, ct=ct, j_=j_: e.tensor_copy(out=state[:, ct:ct + 1], in_=L_h[j_][:, BT - 1:BT]), r=[("Lh", j_)], w=[("state", ct)])
                    if main_:
                        s.op("pool", lambda e, ct=ct, j_=j_: e.tensor_tensor(out=rec[:, ct, :], in0=L_h[j_], in1=gg[:, ct, :], op=ALU.mult),
                             r=[("Lh", j_), ("gg", ct)], w=[("rec", ct)])
                        s.op("pool", lambda e, ct=ct: e.tensor_tensor(out=sq[:, ct, :], in0=rec[:, ct, :], in1=rec[:, ct, :], op=ALU.mult), r=[("rec", ct)], w=[("sq", ct)])

            def lru_norm():
                def mss(e):
                    ins = None
                    for ct in range(4):
                        ins = e.matmul(PS["T"][:, :], lhsT=ones, rhs=sq[:, ct, :], start=(ct == 0), stop=(ct == 3))
                    return ins
                s.op("pe", mss, r=["ones"] + [("sq", ct) for ct in range(4)], x=["T"])
                s.op("act", lambda e: e.activation(out=rstdB, in_=PS["T"][:, :], func=AF.Sqrt, bias=epsb, scale=1.0 / 512.0), r=["epsb"], x=["T"], w=["rstdB"])
                s.op("dve", lambda e: e.reciprocal(out=rstdB, in_=rstdB), r=["rstdB"], w=["rstdB"])
                for ct in range(4):
                    s.op("dve", lambda e, ct=ct: e.scalar_tensor_tensor(out=recb[:, ct, :], in0=rec[:, ct, :], scalar=cols[:, C_GLRU + ct:C_GLRU + ct + 1],
                                                                        in1=rstdB, op0=ALU.mult, op1=ALU.mult),
                         r=[("rec", ct), "rstdB", "cols"], w=[("recb", ct)])

            sbank = ["P0", "P1", "S0", "S1"]
            sc = [0]

            def ATT(b, tt):
                G = 4 * b + tt
                i = tt % 2
                Opair = PSP["D"]
                def QK(ft):
                    pb_ = ft % 2
                    ptk = ("PT", pb_)
                    for grp, rs in enumerate(((0, 1), (2, 3), (4,))):
                        bank = sbank[sc[0] % 4]; sc[0] += 1

                        def fqk(e, rs=rs, bank=bank, ft=ft, G=G, tt=tt):
                            ins = None
                            for li, r in enumerate(rs):
                                ring = (G + r) % 8
                                ins = e.matmul(PS[bank][:, li * 256:(li + 1) * 256].rearrange("p (a b) -> p a b", a=2), lhsT=KT[:, ft, ring * 128:(ring + 1) * 128],
                                               rhs=QP[:, ft, :, tt * 128:(tt + 1) * 128], start=True, stop=True)
                            return ins
                        s.op("pe", fqk, r=[("KT", (G + r) % 8) for r in rs] + [("QP", ft)], x=[bank])
                        ncol = 256 * len(rs)
                        c0 = grp * 512
                        s.op("act", lambda e, bank=bank, pb_=pb_, c0=c0, ncol=ncol: e.activation(out=PTp[pb_][:, c0:c0 + ncol], in_=PS[bank][:, 0:ncol], func=AF.Exp),
                             x=[bank], w=[(ptk, grp)])
                    s.op("dve", lambda e, pb_=pb_, ft=ft: e.tensor_tensor(out=PTp[pb_], in0=PTp[pb_], in1=EB[:, ft, :], op=ALU.mult),
                         r=[(ptk, g_) for g_ in range(3)] + ["EB"], w=[(ptk, g_) for g_ in range(3)])

                def PV(ft):
                    pb_ = ft % 2
                    ptk = ("PT", pb_)
                    for hh in range(2):
                        h = 2 * ft + hh
                        ocol = (h // 4) * 512 + (h % 4) * 65
                        obank = "O0" if h < 4 else "O1"

                        def fpv(e, h=h, hh=hh, ocol=ocol, pb_=pb_, G=G):
                            ins = None
                            for r in range(5):
                                ring = (G + r) % 8
                                ins = e.matmul(Opair[:, ocol:ocol + 65], lhsT=PTp[pb_][:, r * 256 + hh * 128:r * 256 + hh * 128 + 128], rhs=VA[:, ring, h, :],
                                               start=(r == 0), stop=(r == 4))
                            return ins
                        s.op("pe", fpv, r=[(ptk, g_) for g_ in range(3)] + [("VA", (G + r) % 8) for r in range(5)], x=[obank])

                QK(0)
                for ft in range(4):
                    if ft + 1 < 4:
                        QK(ft + 1)
                    PV(ft)
                for bk in range(2):
                    obank = "O0" if bk == 0 else "O1"
                    Ov = PS[obank][:, 0:260].rearrange("p (h d) -> p h d", h=4)
                    rcv = rc[tt][:, 4 * bk:4 * bk + 4].unsqueeze(2)
                    s.op("dve", lambda e, Ov=Ov, rcv=rcv: e.reciprocal(out=rcv, in_=Ov[:, :, 64:65]), x=[obank], w=[("rc", tt, bk)])
                    s.op("dve", lambda e, Ov=Ov, rcv=rcv, bk=bk, tt=tt: e.tensor_tensor(out=att[tt][:, 4 * bk:4 * bk + 4, :], in0=Ov[:, :, 0:64],
                                                                                      in1=rcv.to_broadcast([128, 4, 64]), op=ALU.mult),
                         r=[("rc", tt, bk)], x=[obank], w=[("att", tt, bk)])

            def TAIL(b, tt):
                i = tt % 2
                attf = att[tt].rearrange("p h d -> p (h d)")
                ss, sd, rstd = ast[i]
                s.op("act", lambda e: e.activation(out=attb[i], in_=attf, func=AF.Square, accum_out=ss),
                     r=[("att", tt, 0), ("att", tt, 1)], w=[("attb", i), ("as", i)])
                rstd_pool(ss, 1.0 / 512.0, sd, rstd, ("as", i), ("ar", i))
                s.op("dve", lambda e: e.scalar_tensor_tensor(out=attb[i], in0=attf, scalar=rstd, in1=gatt, op0=ALU.mult, op1=ALU.mult),
                     r=[("att", tt, 0), ("att", tt, 1), ("ar", i), "gatt"], w=[("attb", i)])
            def TAIL_PE(b, tt):
                i = tt % 2
                tp = PS["X"][:, :].bitcast(BF16)

                def tr(e):
                    ins = None
                    for ft in range(4):
                        ins = e.transpose(tp[:, ft * 128:(ft + 1) * 128], attb[i][:, ft * 128:(ft + 1) * 128], ident)
                    return ins
                s.op("pe", tr, r=[("attb", i), "ident"], x=["X"])
                s.op("act", lambda e: e.activation(out=attT[i], in_=tp[:, 0:512].rearrange("p (a b) -> p a b", a=4), func=AF.Copy), x=["X"], w=[("attT", i)])
                r0 = b * BT + tt * 128
                s.dma("sp", lambda e: [e.dma_start(out=xr[i], in_=xsrc[r0:r0 + 128, :])], 1, f"ldr{i}", w=[("xr", i)])
                for hf in range(2):
                    bank = "X" if hf == 0 else "T"

                    def mo(e, hf=hf, bank=bank):
                        ins = None
                        for ft in range(4):
                            e.matmul(PS[bank][:, :], lhsT=attT[i][:, ft, :], rhs=Wout[:, ft, hf * 512:(hf + 1) * 512], start=(ft == 0), stop=False)
                        for ct in range(4):
                            ins = e.matmul(PS[bank][:, :], lhsT=recb[:, ct, tt * 128:(tt + 1) * 128], rhs=Wout[:, 4 + ct, hf * 512:(hf + 1) * 512],
                                           start=False, stop=(ct == 3))
                        return ins
                    s.op("pe", mo, r=["Wout", ("attT", i)] + [("recb", ct) for ct in range(4)], x=[bank])
                    s.op("dve", lambda e, hf=hf, bank=bank: e.tensor_tensor(out=xr[i][:, hf * 512:(hf + 1) * 512], in0=PS[bank][:, :],
                                                                          in1=xr[i][:, hf * 512:(hf + 1) * 512], op=ALU.add),
                         r=[("xr", i)], x=[bank], w=[("xr", i)])
                s.dma("pool", lambda e: [e.dma_start(out=xdst[r0:r0 + 128, :], in_=xr[i])], 1, f"st{i}", r=[("xr", i)], w=[("dst2", r0)])

            nblk = len(blocks)
            sm4 = {0: 0, 1: 1, 2: 2, 3: 3}

            def A_all(bi):
                for tt in range(4):
                    A_pre(bi, tt)
                for tt in range(4):
                    A_tr(bi, tt)

            A_all(0)
            stageB(blocks[0][0], -1)
            lru_front([0, 1, 2, 3], sm4)
            A_all(1)
            for n in range(NBLK):
                def mid(n=n):
                    if n + 1 < NBLK:
                        stageB(blocks[n + 1][0], -1)
                        lru_front([0, 1, 2, 3], sm4)
                lru_back([0, 1, 2, 3], sm4, False, mid)
                if n + 2 <= NBLK:
                    A_all(n + 2)
            s.op("dve", lambda e: e.tensor_scalar(out=state, in0=state, scalar1=flag, scalar2=None, op0=ALU.mult),
                 r=["cols"] + [("state", ct) for ct in range(4)], w=[("state", ct) for ct in range(4)])
            s.barrier()

            for bi, (mode, _, b) in enumerate(blocks):
                nx = bi + 1 if bi + 1 < nblk else None
                if mode != "main":
                    continue
                def ah(h, nx=nx):
                    if nx is not None:
                        A_pre(nx, 2 * h); A_pre(nx, 2 * h + 1)
                sm = {0: 0, 1: 1, 2: 0, 3: 1}
                stageB(mode, b)
                for tt in range(4):
                    if tt >= 1:
                        TAIL(b, tt - 1)
                    ATT(b, tt)
                    if tt < 2:
                        lru_back([2 + tt], sm, True)
                        ah(tt)
                    if tt == 1:
                        lru_norm()
                    if tt >= 2:
                        if nx is not None:
                            A_tr(nx, 2 * (tt - 2)); A_tr(nx, 2 * (tt - 2) + 1)
                        TAIL_PE(b, tt - 2)
                TAIL(b, 3)
                TAIL_PE(b, 2)
                TAIL_PE(b, 3)

        if 2 in phases:
            phase12(srcs[2], dsts[2])
            s.barrier()
        if 0 in phases:
            phase0()
            s.barrier()
        if 3 in phases:
            phase3(srcs[3], dsts[3])
            s.barrier()
        if 4 in phases:
            phase4(srcs[4], dsts[4])

        s.barrier()
        s.op("sp", None)
        s.emit({"pe": block.tensor, "act": block.scalar, "dve": block.vector, "pool": block.gpsimd, "sp": block.sync}, sems)
    return nc, s


def _host_inputs(x, mem, g_mix, w_in, rel_bias, conv_w, conv_b, w_rg, b_rg, w_ig, b_ig, lru_L,
                 g_out_attn, g_out_lru, w_out, g_cross, g_mem, wq_c, wk_c, wv_c, wo_c,
                 g_ffn, w_gate, w_up, w_down, g_final):
    f32 = np.float32
    x = np.asarray(x, f32); mem = np.asarray(mem, f32)
    l = 0

    def col8(v):
        return np.asarray(v, f32).reshape(-1, 128).T

    shared = {}
    cols = np.zeros((128, NCOL), f32)
    cols[:, C_GMIX:C_GMIX + 8] = col8(g_mix[l]); cols[:, C_GCROSS:C_GCROSS + 8] = col8(g_cross[l])
    cols[:, C_GFFN:C_GFFN + 8] = col8(g_ffn[l]); cols[:, C_GMEM:C_GMEM + 8] = col8(g_mem[l])
    cols[:, C_GLRU:C_GLRU + 4] = col8(g_out_lru[l])
    cw = np.asarray(conv_w[l], f32)
    for j in range(4):
        cols[:, C_CONVW + 4 * j:C_CONVW + 4 * j + 4] = col8(cw[j])
    cols[:, C_CONVB:C_CONVB + 4] = col8(conv_b[l]); cols[:, C_BRG:C_BRG + 4] = col8(b_rg[l])
    cols[:, C_BIG:C_BIG + 4] = col8(b_ig[l]); cols[:, C_L:C_L + 4] = col8(lru_L[l])
    kk = np.arange(128)[:, None]; qq = np.arange(128)[None, :]
    bt = np.empty((128, 4, 5, 2, 128), f32)
    rb = np.asarray(rel_bias[l], f32)
    for r in range(5):
        rel = (4 - r) * 128 + qq - kk
        dc = 8 - 2 * r + qq // 64 - kk // 64
        idx = np.clip(rel, -128, 128) + 128
        vis = (dc >= 0) & (dc <= 8)
        for h in range(8):
            bt[:, h // 2, r, h % 2, :] = np.where(vis, rb[h][idx], f32(NEG))
    wrg = np.zeros((4, 128, 128), f32); wig = np.zeros((4, 128, 128), f32)
    for ct in range(4):
        for q in range(2):
            wrg[ct, q * 64:(q + 1) * 64, q * 64:(q + 1) * 64] = np.asarray(w_rg[l][2 * ct + q], f32)
            wig[ct, q * 64:(q + 1) * 64, q * 64:(q + 1) * 64] = np.asarray(w_ig[l][2 * ct + q], f32)
    shared.update({
        "g_final": np.asarray(g_final, f32), "g_out_attn": np.asarray(g_out_attn[l], f32),
        "g_mix": np.asarray(g_mix[l], f32), "g_cross": np.asarray(g_cross[l], f32), "g_ffn": np.asarray(g_ffn[l], f32), "g_mem": np.asarray(g_mem[l], f32),
        "ident": np.eye(128, dtype=f32), "bias_t": bt.reshape(128, -1),
        "w_in": np.asarray(w_in[l], f32), "w_out": np.asarray(w_out[l], f32), "wbd_rg": wrg, "wbd_ig": wig,
        "wq_c": np.asarray(wq_c[l], f32), "wk_c": np.asarray(wk_c[l], f32), "wv_c": np.asarray(wv_c[l], f32),
        "wo_c": np.asarray(wo_c[l], f32), "w_gate": np.asarray(w_gate[l], f32), "w_up": np.asarray(w_up[l], f32),
        "w_down": np.asarray(w_down[l], f32),
    })
    in_maps = []
    zeros = np.zeros((T, D), f32)
    for c in range(8):
        b, hf = c // 2, c % 2
        cc = cols.copy(); cc[:, C_FLAG] = float(hf)
        m = dict(shared)
        m["x_main"] = np.ascontiguousarray(x[b, hf * T:(hf + 1) * T])
        m["x_prev"] = np.ascontiguousarray(x[b, 0:T]) if hf == 1 else zeros
        m["mem"] = np.ascontiguousarray(mem[b])
        m["cols"] = cc
        in_maps.append(m)
    return in_maps


_CACHE = {}


def kernel(**inputs):
    in_maps = _host_inputs(**inputs)
    if "nc" not in _CACHE:
        _CACHE["nc"] = build()[0]
    res = run_bass_kernel_spmd(_CACHE["nc"], in_maps, core_ids=list(range(8)))
    out = np.empty((4, 2 * T, D), np.float32)
    for c in range(8):
        out[c // 2, (c % 2) * T:(c % 2 + 1) * T] = res.results[c]["out"]
    return out
```
